# Optimizing a Trainium2 kernel written in Bass

```python
import jax, jax.numpy as jnp
from jax import lax
import numpy as np

D_MODEL = 1024
BATCH = 8
SEQ = 2048
DEPTH = 4

CTX_LEN = 256
GRID_W = 64
HEAD_DIM = 64
N_MOD = 9
D_FF = 2816
EPS = 1e-6
NEG_INF = -1e30
NA_HEADS = 4
NA_WIN_H_MAX = 8
NA_WIN_W = 16
NA_QCB = 16
NA_KCB = 32
NA_WIDTH = NA_HEADS * HEAD_DIM
POOL_WINDOWS = (2, 4, 8, 16)
POOL_GROUPS = 4
POOL_WIDTH = 256
POOL_GC = POOL_WIDTH // POOL_GROUPS
GQA_Q_HEADS = 8
GQA_KV_HEADS = 2
GQA_GROUP = GQA_Q_HEADS // GQA_KV_HEADS
GQA_QB = 128
GQA_Q_WIDTH = GQA_Q_HEADS * HEAD_DIM
GQA_KV_WIDTH = GQA_KV_HEADS * HEAD_DIM
ROPE_THETA = 10000.0
OFF_A_Q = 0
OFF_A_K = OFF_A_Q + NA_WIDTH
OFF_A_V = OFF_A_K + NA_WIDTH
OFF_B_U = OFF_A_V + NA_WIDTH
OFF_C_Q = OFF_B_U + POOL_WIDTH
OFF_C_K = OFF_C_Q + GQA_Q_WIDTH
OFF_C_V = OFF_C_K + GQA_KV_WIDTH
D_IN = OFF_C_V + GQA_KV_WIDTH
D_MIX = NA_WIDTH + POOL_WIDTH + GQA_Q_WIDTH

kernel_name = 'hybrid_na_pool_gqa_macaron_dit'


def _rmsnorm(x, g):
    x32 = x.astype(jnp.float32)
    y = x32 * lax.rsqrt(jnp.mean(x32 * x32, axis=-1, keepdims=True) + EPS)
    return (y * g.astype(jnp.float32)).astype(x.dtype)


def _modulate(h, shift, scale):
    return h * (1 + scale) + shift


def _swiglu(h, w_up, w_down):
    a, b = jnp.split(h @ w_up, 2, axis=-1)
    return (jax.nn.silu(a) * b) @ w_down


def _half_ffn(h, g, shift, scale, gate, w_up, w_down):
    return h + 0.5 * gate * _swiglu(_modulate(_rmsnorm(h, g), shift, scale), w_up, w_down)


def _rope_axis(x, ang):
    m = x.shape[-1] // 2
    x32 = x.astype(jnp.float32)
    x1, x2 = x32[..., :m], x32[..., m:]
    cos = jnp.cos(ang)[:, None, :]
    sin = jnp.sin(ang)[:, None, :]
    return jnp.concatenate([x1 * cos - x2 * sin, x1 * sin + x2 * cos], axis=-1)


def _rope_2d(x, ang_row, ang_col):
    half = x.shape[-1] // 2
    out = jnp.concatenate([_rope_axis(x[..., :half], ang_row), _rope_axis(x[..., half:], ang_col)], axis=-1)
    return out.astype(x.dtype)


def _gqa_attend(q, k, v):
    s = jnp.einsum('bqkgd,bskd->bkgqs', q, k, preferred_element_type=jnp.float32) * (q.shape[-1] ** -0.5)
    p = jax.nn.softmax(s, axis=-1).astype(v.dtype)
    return jnp.einsum('bkgqs,bskd->bqkgd', p, v)


def _gqa_blocks(q, k_all, v_all):
    bn, n = q.shape[0], q.shape[1]
    nqb = n // GQA_QB
    qb = q.reshape(bn, nqb, GQA_QB, GQA_KV_HEADS, GQA_GROUP, HEAD_DIM).transpose(1, 0, 2, 3, 4, 5)
    o = lax.map(lambda blk: _gqa_attend(blk, k_all, v_all), qb)
    return o.transpose(1, 0, 2, 3, 4, 5).reshape(bn, n, GQA_Q_WIDTH)


def _neighbourhood_attn(q, k, v, k_ctx, v_ctx, rpb):
    bn, n, nh, dh = q.shape
    rows = n // GRID_W
    kh = min(NA_WIN_H_MAX, rows)
    ncb = GRID_W // NA_QCB
    col = jnp.arange(GRID_W)
    win_c0 = jnp.clip(col - NA_WIN_W // 2, 0, GRID_W - NA_WIN_W)
    blk_c0 = jnp.clip(win_c0[::NA_QCB], 0, GRID_W - NA_KCB)
    key_col = blk_c0[:, None] + jnp.arange(NA_KCB)
    q_c0 = win_c0.reshape(ncb, NA_QCB)[:, :, None]
    kc = key_col[:, None, :]
    valid = (kc >= q_c0) & (kc < q_c0 + NA_WIN_W)
    dx = kc - col.reshape(ncb, NA_QCB)[:, :, None]
    dx_idx = jnp.clip(dx, -(NA_WIN_W - 1), NA_WIN_W - 1) + NA_WIN_W - 1
    scale = dh ** -0.5
    q_grid = q.reshape(bn, rows, ncb, NA_QCB, nh, dh)

    def one_row(r):
        q_r = lax.dynamic_index_in_dim(q_grid, r, axis=1, keepdims=False)
        r0 = jnp.clip(r - kh // 2, 0, rows - kh)
        key_row = r0 + jnp.arange(kh)
        tok = key_row[None, :, None] * GRID_W + key_col[:, None, :]
        k_blk = k[:, tok]
        v_blk = v[:, tok]
        s_lat = jnp.einsum('bjqhd,bjykhd->bhjqyk', q_r, k_blk, preferred_element_type=jnp.float32) * scale
        dy_idx = key_row - r + NA_WIN_H_MAX - 1
        bias = rpb[:, dy_idx[None, None, :, None], dx_idx[:, :, None, :]]
        s_lat = jnp.where(valid[:, :, None, :], s_lat + bias.astype(jnp.float32), NEG_INF)
        s_lat = s_lat.reshape(bn, nh, ncb, NA_QCB, kh * NA_KCB)
        s_ctx = jnp.einsum('bjqhd,bchd->bhjqc', q_r, k_ctx, preferred_element_type=jnp.float32) * scale
        p = jax.nn.softmax(jnp.concatenate([s_lat, s_ctx], axis=-1), axis=-1).astype(v.dtype)
        p_lat = p[..., :kh * NA_KCB].reshape(bn, nh, ncb, NA_QCB, kh, NA_KCB)
        p_ctx = p[..., kh * NA_KCB:]
        o = jnp.einsum('bhjqyk,bjykhd->bjqhd', p_lat, v_blk) + jnp.einsum('bhjqc,bchd->bjqhd', p_ctx, v_ctx)
        return o.reshape(bn, GRID_W, nh, dh)

    o = lax.map(one_row, jnp.arange(rows))
    return o.transpose(1, 0, 2, 3, 4).reshape(bn, n, nh * dh)


def _pool_mix(u, w_pool, scale):
    bn, n, _ = u.shape
    u32 = u.astype(jnp.float32)
    cs = jnp.concatenate([jnp.zeros_like(u32[:, :1]), jnp.cumsum(u32, axis=1)], axis=1)
    cs = cs.reshape(bn, n + 1, POOL_GROUPS, POOL_GC)
    t = jnp.arange(n)[:, None]
    win = jnp.array(POOL_WINDOWS, dtype=jnp.int32)[None, :]
    lo = jnp.clip(t - win // 2, 0, n)
    hi = jnp.clip(t - win // 2 + win, 0, n)
    g_idx = jnp.arange(POOL_GROUPS)[None, :]
    sums = cs[:, hi, g_idx] - cs[:, lo, g_idx]
    mean = sums / (hi - lo).astype(jnp.float32)[None, :, :, None]
    y = (mean - u32.reshape(bn, n, POOL_GROUPS, POOL_GC)).astype(u.dtype)
    y = jnp.einsum('blgc,gcd->blgd', y, w_pool)
    return y.reshape(bn, n, POOL_WIDTH) * scale


def setup_inputs(seed: int = 0) -> dict:
    key = jax.random.key(seed)
    ks = jax.random.split(key, 20)
    nrm = jax.random.normal
    f32 = jnp.float32
    return {
        'x': nrm(ks[0], (BATCH, SEQ, D_MODEL), f32),
        'c': nrm(ks[1], (BATCH, D_MODEL), f32),
        'ctx': nrm(ks[2], (BATCH, CTX_LEN, D_MODEL), f32),
        'c_ctx': nrm(ks[3], (D_MODEL,), f32),
        'w_ada': nrm(ks[4], (DEPTH, D_MODEL, N_MOD * D_MODEL), f32) * D_MODEL ** -0.5,
        'b_ada': nrm(ks[5], (DEPTH, N_MOD * D_MODEL), f32) * 0.02,
        'norm_g': 1.0 + 0.05 * nrm(ks[6], (DEPTH, 3, D_MODEL), f32),
        'ffn1_up': nrm(ks[7], (DEPTH, D_MODEL, 2 * D_FF), f32) * D_MODEL ** -0.5,
        'ffn1_down': nrm(ks[8], (DEPTH, D_FF, D_MODEL), f32) * D_FF ** -0.5,
        'ffn2_up': nrm(ks[9], (DEPTH, D_MODEL, 2 * D_FF), f32) * D_MODEL ** -0.5,
        'ffn2_down': nrm(ks[10], (DEPTH, D_FF, D_MODEL), f32) * D_FF ** -0.5,
        'w_in': nrm(ks[11], (DEPTH, D_MODEL, D_IN), f32) * D_MODEL ** -0.5,
        'w_out': nrm(ks[12], (DEPTH, D_MIX, D_MODEL), f32) * D_MIX ** -0.5,
        'na_rpb': nrm(ks[13], (DEPTH, NA_HEADS, 2 * NA_WIN_H_MAX - 1, 2 * NA_WIN_W - 1), f32) * 0.1,
        'pool_w': nrm(ks[14], (DEPTH, POOL_GROUPS, POOL_GC, POOL_GC), f32) * POOL_GC ** -0.5,
        'pool_scale': 1.0 + 0.1 * nrm(ks[15], (DEPTH, POOL_WIDTH), f32),
        'q_norm_g': 1.0 + 0.05 * nrm(ks[16], (DEPTH, HEAD_DIM), f32),
        'k_norm_g': 1.0 + 0.05 * nrm(ks[17], (DEPTH, HEAD_DIM), f32),
        'final_g': 1.0 + 0.05 * nrm(ks[18], (D_MODEL,), f32),
    }


def reference(x, c, ctx, c_ctx, w_ada, b_ada, norm_g, ffn1_up, ffn1_down, ffn2_up, ffn2_down,
              w_in, w_out, na_rpb, pool_w, pool_scale, q_norm_g, k_norm_g, final_g):
    bn, n, d = x.shape
    ncx = ctx.shape[1]
    pos = jnp.arange(n)
    inv_freq = ROPE_THETA ** (-jnp.arange(0, HEAD_DIM // 2, 2, dtype=jnp.float32) / (HEAD_DIM // 2))
    ang_row = (pos // GRID_W).astype(jnp.float32)[:, None] * inv_freq[None, :]
    ang_col = (pos % GRID_W).astype(jnp.float32)[:, None] * inv_freq[None, :]
    h_x, h_c = x, ctx
    for l in range(DEPTH):
        last = l == DEPTH - 1
        mod_x = (jax.nn.silu(c) @ w_ada[l] + b_ada[l]).reshape(bn, 1, N_MOD, d)
        mod_c = (jax.nn.silu(c_ctx) @ w_ada[l] + b_ada[l]).reshape(1, 1, N_MOD, d)
        h_x = _half_ffn(h_x, norm_g[l, 0], mod_x[:, :, 0], mod_x[:, :, 1], mod_x[:, :, 2], ffn1_up[l], ffn1_down[l])
        h_c = _half_ffn(h_c, norm_g[l, 0], mod_c[:, :, 0], mod_c[:, :, 1], mod_c[:, :, 2], ffn1_up[l], ffn1_down[l])
        a_x = _modulate(_rmsnorm(h_x, norm_g[l, 1]), mod_x[:, :, 3], mod_x[:, :, 4])
        a_c = _modulate(_rmsnorm(h_c, norm_g[l, 1]), mod_c[:, :, 3], mod_c[:, :, 4])
        p_x = a_x @ w_in[l]
        kv_a_c = a_c @ w_in[l][:, OFF_A_K:OFF_B_U]
        kv_c_c = a_c @ w_in[l][:, OFF_C_K:D_IN]
        q_a = p_x[..., OFF_A_Q:OFF_A_K].reshape(bn, n, NA_HEADS, HEAD_DIM)
        k_a = p_x[..., OFF_A_K:OFF_A_V].reshape(bn, n, NA_HEADS, HEAD_DIM)
        v_a = p_x[..., OFF_A_V:OFF_B_U].reshape(bn, n, NA_HEADS, HEAD_DIM)
        k_a_c = kv_a_c[..., :NA_WIDTH].reshape(bn, ncx, NA_HEADS, HEAD_DIM)
        v_a_c = kv_a_c[..., NA_WIDTH:].reshape(bn, ncx, NA_HEADS, HEAD_DIM)
        q_c = _rope_2d(_rmsnorm(p_x[..., OFF_C_Q:OFF_C_K].reshape(bn, n, GQA_Q_HEADS, HEAD_DIM), q_norm_g[l]), ang_row, ang_col)
        q_c = q_c.reshape(bn, n, GQA_KV_HEADS, GQA_GROUP, HEAD_DIM)
        k_c = _rope_2d(_rmsnorm(p_x[..., OFF_C_K:OFF_C_V].reshape(bn, n, GQA_KV_HEADS, HEAD_DIM), k_norm_g[l]), ang_row, ang_col)
        v_c = p_x[..., OFF_C_V:D_IN].reshape(bn, n, GQA_KV_HEADS, HEAD_DIM)
        k_c_c = _rmsnorm(kv_c_c[..., :GQA_KV_WIDTH].reshape(bn, ncx, GQA_KV_HEADS, HEAD_DIM), k_norm_g[l])
        v_c_c = kv_c_c[..., GQA_KV_WIDTH:].reshape(bn, ncx, GQA_KV_HEADS, HEAD_DIM)
        o_a = _neighbourhood_attn(q_a, k_a, v_a, k_a_c, v_a_c, na_rpb[l])
        o_b = _pool_mix(p_x[..., OFF_B_U:OFF_C_Q], pool_w[l], pool_scale[l])
        o_c = _gqa_blocks(q_c, jnp.concatenate([k_c, k_c_c], axis=1), jnp.concatenate([v_c, v_c_c], axis=1))
        h_x = h_x + mod_x[:, :, 5] * (jnp.concatenate([o_a, o_b, o_c], axis=-1) @ w_out[l])
        h_x = _half_ffn(h_x, norm_g[l, 2], mod_x[:, :, 6], mod_x[:, :, 7], mod_x[:, :, 8], ffn2_up[l], ffn2_down[l])
        if not last:
            q_a_c = (a_c @ w_in[l][:, OFF_A_Q:OFF_A_K]).reshape(bn, ncx, NA_HEADS, 1, HEAD_DIM)
            u_c = a_c @ w_in[l][:, OFF_B_U:OFF_C_Q]
            q_c_c = _rmsnorm((a_c @ w_in[l][:, OFF_C_Q:OFF_C_K]).reshape(bn, ncx, GQA_Q_HEADS, HEAD_DIM), q_norm_g[l])
            q_c_c = q_c_c.reshape(bn, ncx, GQA_KV_HEADS, GQA_GROUP, HEAD_DIM)
            oa_c = _gqa_attend(q_a_c, k_a_c, v_a_c).reshape(bn, ncx, NA_WIDTH)
            ob_c = _pool_mix(u_c, pool_w[l], pool_scale[l])
            oc_c = _gqa_attend(q_c_c, k_c_c, v_c_c).reshape(bn, ncx, GQA_Q_WIDTH)
            h_c = h_c + mod_c[:, :, 5] * (jnp.concatenate([oa_c, ob_c, oc_c], axis=-1) @ w_out[l])
            h_c = _half_ffn(h_c, norm_g[l, 2], mod_c[:, :, 6], mod_c[:, :, 7], mod_c[:, :, 8], ffn2_up[l], ffn2_down[l])
    return _rmsnorm(h_x, final_g)
```

```python
import math
from contextlib import ExitStack

import numpy as np
import concourse.bass as bass
import concourse.mybir as mybir
from concourse.bass_utils import run_bass_kernel_spmd

F32 = mybir.dt.float32
BF16 = mybir.dt.bfloat16
ALU = mybir.AluOpType
AF = mybir.ActivationFunctionType

D = 1024
T = 2304
NXT = 2048
NCT = 256
L = 4
DFF = 2816
EPS = 1e-6
BLKS = [(0, 512), (512, 512), (1024, 512), (1536, 512), (2048, 256)]
NEGB = -30000.0


class Sem:
    def __init__(self, nc, es, name):
        self.sem = es.enter_context(nc.semaphore(name))
        self.count = 0


class Buf:
    __slots__ = ("w", "r")

    def __init__(self):
        self.w = None
        self.r = {}


class KB:
    def __init__(self, nc, es):
        self.nc = nc
        self.eng = {"pe": nc.tensor, "act": nc.scalar, "dve": nc.vector, "pool": nc.gpsimd, "sp": nc.sync}
        self.S = {n: Sem(nc, es, "s_" + n) for n in ["pe", "act", "dve", "pool"]}
        self.seen = {n: {} for n in self.eng}
        self.chans = {q: [Sem(nc, es, "c_%s%d" % (q, i)) for i in range(8)] for q in ["sp", "pool"]}
        self.chan_i = {"sp": 0, "pool": 0}

    def _wait(self, en, deps):
        for S, c in deps.items():
            if en == "pe" and S is self.S["pe"]:
                continue
            if self.seen[en].get(S, 0) < c:
                assert c <= S.count, "wait on future count"
                self.eng[en].wait_ge(S.sem, c)
                self.seen[en][S] = c

    @staticmethod
    def _deps(reads, writes):
        d = {}
        for b in reads:
            if b.w is not None and d.get(b.w[0], 0) < b.w[1]:
                d[b.w[0]] = b.w[1]
        for b in writes:
            if b.w is not None and d.get(b.w[0], 0) < b.w[1]:
                d[b.w[0]] = b.w[1]
            for S, c in b.r.items():
                if d.get(S, 0) < c:
                    d[S] = c
        return d

    def op(self, en, fn, reads=(), writes=(), inc=True):
        self._wait(en, self._deps(reads, writes))
        ins = fn(self.eng[en])
        S = self.S[en]
        if inc:
            S.count += 1
            ins.then_inc(S.sem, 1)
            st = (S, S.count)
        else:
            st = (S, S.count + 1)
        for b in reads:
            if b.r.get(S, 0) < st[1]:
                b.r[S] = st[1]
        for b in writes:
            b.w = st
            b.r = {}
        return ins

    def dma(self, q, out, in_, reads=(), writes=()):
        chs = self.chans[q]
        ch = chs[self.chan_i[q] % len(chs)]
        self.chan_i[q] += 1
        d = self._deps(reads, writes)
        if ch.count > 0 and d.get(ch, 0) < ch.count:
            d[ch] = ch.count
        self._wait(q, d)
        ins = self.eng[q].dma_start(out=out, in_=in_)
        ch.count += 16
        ins.then_inc(ch.sem, 16)
        for b in reads:
            if b.r.get(ch, 0) < ch.count:
                b.r[ch] = ch.count
        for b in writes:
            b.w = (ch, ch.count)
            b.r = {}

    def barrier(self):
        allS = list(self.S.values()) + self.chans["sp"] + self.chans["pool"]
        for en in self.eng:
            self._wait(en, {S: S.count for S in allS if S.count > 0})


class Tl:
    def __init__(self, t):
        self.t = t
        self.b = {}

    def __getitem__(self, k):
        return self.t[k]

    def buf(self, k=0):
        if k not in self.b:
            self.b[k] = Buf()
        return self.b[k]


def build_program(n_layers=L, dbg=False, stop_after=None):
    nc = bass.Bass("TRN2", target_bir_lowering=False)

    def din(name, shape, dt=F32):
        return nc.dram_tensor(name, shape, dt, kind="ExternalInput").ap()

    xin = din("xin", [T, D])
    cc_d = din("cc", [128, 8, 2])
    wada_d = din("w_ada", [L, D, 9 * D])
    bada_d = din("bada", [128, L * 72])
    normg_d = din("normg", [128, L * 24])
    f1u_d = din("ffn1_up", [L, D, 2 * DFF])
    f1d_d = din("ffn1_down", [L, DFF, D])
    f2u_d = din("ffn2_up", [L, D, 2 * DFF])
    f2d_d = din("ffn2_down", [L, DFF, D])
    win_d = din("w_in", [L, D, 1792])
    wout_d = din("w_out", [L, D, D])
    rpbx_d = din("rpbx", [L, 4, 64, 15, 64])
    poolw_d = din("pool_w", [L, 4, 64, 64])
    pscale_d = din("pscale", [128, L * 2])
    qkg_d = din("qkg", [128, L * 2])
    fg_d = din("fg", [128, 8])
    ident_d = din("ident", [128, 128])
    rotm_d = din("rotm", [128, 128])
    hm_d = din("hm", [128, 128])
    cos_d = din("cosT", [128, T])
    sin_d = din("sinT", [128, T])
    rmask_d = din("rmask", [128, 12 * 8])
    pcnt_d = din("pcnt", [128, 2 * 2 * 16])
    out_d = nc.dram_tensor("out", [NXT, D], F32, kind="ExternalOutput").ap()
    QKn = Tl(nc.dram_tensor("s_qkn", [4, 128, T], BF16, kind="Internal").ap())
    Usc = Tl(nc.dram_tensor("s_u", [2, 128, T], F32, kind="Internal").ap())
    QCs = Tl(nc.dram_tensor("s_qc", [4, 128, T], BF16, kind="Internal").ap())
    KCs = Tl(nc.dram_tensor("s_kc", [128, T], BF16, kind="Internal").ap())
    VAs = Tl(nc.dram_tensor("s_va", [18, 128, 6, 128], BF16, kind="Internal").ap())
    OTs = Tl(nc.dram_tensor("s_ot", [8, 128, T], BF16, kind="Internal").ap())
    if dbg:
        hd_d = nc.dram_tensor("hdump", [128, 8, T], F32, kind="ExternalOutput").ap()

    with ExitStack() as es:
        kb = KB(nc, es)

        uid = [0]

        def sbt(st, name, shape, dt):
            uid[0] += 1
            return Tl(st.enter_context(nc.sbuf_tensor("sb%d_%s" % (uid[0], name), shape, dt)))

        ps = Tl(es.enter_context(nc.psum_tensor("ps", [128, 8, 512], F32)))

        def bank(b):
            return ps.buf(b)

        h = sbt(es, "h", [128, 8, T], F32)
        wu = sbt(es, "wu", [128, 2, 8, 2, 256], BF16)
        wd = sbt(es, "wd", [128, 2, 2, D], BF16)
        mw = sbt(es, "mw", [128, 2, 8, 256], BF16)
        modraw = sbt(es, "modraw", [128, 72, 2], F32)
        modv = sbt(es, "modv", [128, 2, 72], F32)
        mder = sbt(es, "mder", [128, 2, 3, 24], F32)
        sc = sbt(es, "sc", [128, 8, 2], BF16)
        ccs = sbt(es, "ccs", [128, 8, 2], F32)
        bada = sbt(es, "bada", [128, L * 72], F32)
        normg = sbt(es, "normg", [128, L * 24], F32)
        pscale = sbt(es, "pscale", [128, L * 2], F32)
        qkg = sbt(es, "qkg", [128, L * 2], F32)
        fg = sbt(es, "fg", [128, 8], F32)
        ident = sbt(es, "ident", [128, 128], F32)
        rotm = sbt(es, "rotm", [128, 128], F32)
        hm = sbt(es, "hm", [128, 128], BF16)
        ones_bf = sbt(es, "ones_bf", [128, 128], BF16)
        rmask = sbt(es, "rmask", [128, 12, 8], BF16)
        prc = sbt(es, "prc", [128, 2, 2, 16], F32)
        epsc = sbt(es, "epsc", [128, 1], F32)

        def hb(s, bi):
            return h.buf((s, bi))

        for tl, dr in [(bada, bada_d), (normg, normg_d), (pscale, pscale_d), (qkg, qkg_d), (fg, fg_d),
                       (ident, ident_d), (rotm, rotm_d), (ccs, cc_d)]:
            kb.dma("sp", tl[:], dr, writes=[tl.buf()])
        kb.dma("pool", hm[:], hm_d, writes=[hm.buf()])
        kb.dma("pool", rmask[:].rearrange("p a b -> p (a b)"), rmask_d, writes=[rmask.buf()])
        kb.dma("sp", prc[:].rearrange("p a b c -> p (a b c)"), pcnt_d, writes=[prc.buf()])
        kb.op("dve", lambda e: e.reciprocal(prc[:], prc[:]), reads=[prc.buf()], writes=[prc.buf()])
        kb.op("dve", lambda e: e.memset(ones_bf[:], 1.0), writes=[ones_bf.buf()])
        kb.op("dve", lambda e: e.memset(epsc[:], EPS), writes=[epsc.buf()])
        kb.op("act", lambda e: e.activation(sc[:], ccs[:], AF.Silu), reads=[ccs.buf()], writes=[sc.buf()])

        def mod_dma(l, pi):
            slot = pi % 2
            src = wada_d[l].rearrange("(k p) n -> p k n", p=128)[:, :, pi * 256:(pi + 1) * 256]
            kb.dma("pool", mw[:, slot], src, writes=[mw.buf(slot)])

        def mod_mm(l, pi):
            slot = pi % 2
            for c in range(2):
                for k in range(8):
                    kb.op("pe", lambda e, c=c, k=k: e.matmul(ps[:, 7, c * 2:c * 2 + 2],
                                                              mw[:, slot, k, c * 128:(c + 1) * 128], sc[:, k, :],
                                                              start=(k == 0), stop=(k == 7)),
                          reads=[mw.buf(slot), sc.buf()], writes=[bank(7)], inc=(c == 1 and k == 7))
            kb.op("dve", lambda e: e.tensor_copy(modraw[:, 2 * pi:2 * pi + 2, :],
                                                 ps[:, 7, 0:4].rearrange("p (a b) -> p a b", a=2)),
                  reads=[bank(7)], writes=[modraw.buf()])

        def emit_mod_piece(l, pi):
            mod_dma(l, pi)
            mod_mm(l, pi)

        def finalize_mod(l):
            for v in range(2):
                kb.op("dve", lambda e, v=v: e.tensor_tensor(modv[:, v, :], modraw[:, :, v], bada[:, l * 72:(l + 1) * 72],
                                                            ALU.add),
                      reads=[modraw.buf(), bada.buf()], writes=[modv.buf()])
            for v in range(2):
                for n in range(3):
                    kb.op("dve", lambda e, v=v, n=n: e.scalar_tensor_tensor(
                        mder[:, v, 0, n * 8:(n + 1) * 8], modv[:, v, (3 * n + 1) * 8:(3 * n + 2) * 8], 1.0,
                        normg[:, (l * 3 + n) * 8:(l * 3 + n + 1) * 8], ALU.add, ALU.mult),
                        reads=[modv.buf(), normg.buf()], writes=[mder.buf()])
                    kb.op("dve", lambda e, v=v, n=n: e.tensor_copy(mder[:, v, 1, n * 8:(n + 1) * 8],
                                                                   modv[:, v, (3 * n) * 8:(3 * n + 1) * 8]),
                          reads=[modv.buf()], writes=[mder.buf()])
                    kb.op("dve", lambda e, v=v, n=n: e.tensor_scalar(
                        mder[:, v, 2, n * 8:(n + 1) * 8], modv[:, v, (3 * n + 2) * 8:(3 * n + 3) * 8],
                        (1.0 if n == 1 else 0.5), None, ALU.mult),
                        reads=[modv.buf()], writes=[mder.buf()])

        def mscal(bi, kind, n, s):
            v = 1 if bi == 4 else 0
            return mder[:, v, kind, n * 8 + s:n * 8 + s + 1]

        with ExitStack() as ph:
            xt = sbt(ph, "xt", [128, 2, D], F32)
            for tt in range(18):
                sl = tt % 2
                kb.dma("sp", xt[:, sl, :], xin[tt * 128:(tt + 1) * 128, :], writes=[xt.buf(sl)])
                for hf in range(2):
                    bk = (tt * 2 + hf) % 6
                    for s4 in range(4):
                        s = hf * 4 + s4
                        kb.op("pe", lambda e, s=s, s4=s4, bk=bk: e.transpose(ps[:, bk, s4 * 128:(s4 + 1) * 128],
                                                                             xt[:, sl, s * 128:(s + 1) * 128], ident[:]),
                              reads=[xt.buf(sl), ident.buf()], writes=[bank(bk)], inc=(s4 == 3))
                    bi = min(tt // 4, 4)
                    eng = "dve" if hf == 0 else "act"
                    outap = h[:, hf * 4:hf * 4 + 4, tt * 128:(tt + 1) * 128]
                    inap = ps[:, bk, :].rearrange("p (a b) -> p a b", a=4)
                    if eng == "dve":
                        kb.op("dve", lambda e, o=outap, i=inap: e.tensor_copy(o, i), reads=[bank(bk)],
                              writes=[hb(s, bi) for s in range(hf * 4, hf * 4 + 4)])
                    else:
                        kb.op("act", lambda e, o=outap, i=inap: e.activation(o, i, AF.Copy), reads=[bank(bk)],
                              writes=[hb(s, bi) for s in range(hf * 4, hf * 4 + 4)])
            for pi in range(36):
                emit_mod_piece(0, pi)
            finalize_mod(0)
            kb.barrier()

        def stats(st, n_blocks):
            rstd = sbt(st, "rstd", [128, T], F32)
            sq = sbt(st, "sq", [128, 1, 8, 512], BF16)
            for bi in range(n_blocks):
                t0, w = BLKS[bi]
                sl = 0
                kb.op("act", lambda e: e.activation(sq[:, sl, :, 0:w], h[:, :, t0:t0 + w], AF.Square),
                      reads=[hb(s, bi) for s in range(8)], writes=[sq.buf(sl)])
                bk = 4 + bi % 2
                for k in range(8):
                    kb.op("pe", lambda e, k=k: e.matmul(ps[:, bk, 0:w], ones_bf[:], sq[:, sl, k, 0:w],
                                                        start=(k == 0), stop=(k == 7)),
                          reads=[sq.buf(sl), ones_bf.buf()], writes=[bank(bk)], inc=(k == 7))
                kb.op("dve", lambda e: e.tensor_scalar(rstd[:, t0:t0 + w], ps[:, bk, 0:w], 1.0 / D, EPS, ALU.mult, ALU.add),
                      reads=[bank(bk)], writes=[rstd.buf(bi)])
            tw = BLKS[n_blocks - 1][0] + BLKS[n_blocks - 1][1]
            rb = [rstd.buf(bi) for bi in range(n_blocks)]
            kb.op("act", lambda e: e.activation(rstd[:, 0:tw], rstd[:, 0:tw], AF.Sqrt), reads=rb, writes=rb)
            kb.op("dve", lambda e: e.reciprocal(rstd[:, 0:tw], rstd[:, 0:tw]), reads=rb, writes=rb)
            return rstd

        def norm_block(bi, n, rstd, dst_fn, dst_bufs, tmp):
            t0, w = BLKS[bi]
            for s in range(8):
                sl = s % 2
                kb.op("dve", lambda e, s=s, sl=sl: e.scalar_tensor_tensor(
                    tmp[:, sl, 0:w], h[:, s, t0:t0 + w], mscal(bi, 0, n, s), rstd[:, t0:t0 + w], ALU.mult, ALU.mult),
                    reads=[hb(s, bi), rstd.buf(bi), mder.buf()], writes=[tmp.buf(sl)])
                kb.op("act", lambda e, s=s, sl=sl: e.activation(dst_fn(s), tmp[:, sl, 0:w], AF.Identity,
                                                                bias=mscal(bi, 1, n, s), scale=1.0),
                      reads=[tmp.buf(sl), mder.buf()], writes=[dst_bufs(s)])

        prefetched = set()

        def prefetch_ffn(l_, which_):
            nup = (f1u_d if which_ == 1 else f2u_d)[l_].rearrange("(k p) n -> p k n", p=128)
            ndn = (f1d_d if which_ == 1 else f2d_d)[l_].rearrange("(c p) n -> p c n", p=128)
            for gi in range(2):
                slot = gi % 2
                kb.dma("pool", wu[:, slot, :, 0, :], nup[:, :, gi * 256:(gi + 1) * 256], writes=[wu.buf((slot, 0))])
                kb.dma("pool", wu[:, slot, :, 1, :], nup[:, :, DFF + gi * 256:DFF + (gi + 1) * 256],
                       writes=[wu.buf((slot, 1))])
                kb.dma("pool", wd[:, slot], ndn[:, 2 * gi:2 * gi + 2, :], writes=[wd.buf(slot)])
            prefetched.add((l_, which_))

        def ffn_phase(l, which, n_blocks, next_mod_layer):
            n = 0 if which == 1 else 2
            up_d = (f1u_d if which == 1 else f2u_d)[l].rearrange("(k p) n -> p k n", p=128)
            dn_d = (f1d_d if which == 1 else f2d_d)[l].rearrange("(c p) n -> p c n", p=128)
            with ExitStack() as ph:
                rstd = stats(ph, n_blocks)
                xn = sbt(ph, "xn", [128, 8, T], BF16)
                tmp = sbt(ph, "ntmp", [128, 2, 512], F32)
                act = sbt(ph, "act", [128, 2, 2, 512], BF16)
                sil = sbt(ph, "sil", [128, 2, 512], F32)
                for bi in range(n_blocks):
                    t0, w = BLKS[bi]
                    norm_block(bi, n, rstd, lambda s: xn[:, s, t0:t0 + w], lambda s: xn.buf((s, bi)), tmp)

                def load_group(gi, up_=None, dn_=None):
                    up_ = up_d if up_ is None else up_
                    dn_ = dn_d if dn_ is None else dn_
                    slot = gi % 2
                    kb.dma("pool", wu[:, slot, :, 0, :], up_[:, :, gi * 256:(gi + 1) * 256], writes=[wu.buf((slot, 0))])
                    kb.dma("pool", wu[:, slot, :, 1, :], up_[:, :, DFF + gi * 256:DFF + (gi + 1) * 256],
                           writes=[wu.buf((slot, 1))])
                    kb.dma("pool", wd[:, slot], dn_[:, 2 * gi:2 * gi + 2, :], writes=[wd.buf(slot)])

                steps = [(gi, bi) for gi in range(11) for bi in range(n_blocks)]

                def up_gen(i):
                    gi, bi = steps[i]
                    slot = gi % 2
                    t0, w = BLKS[bi]
                    pp = i % 2
                    for c in range(2):
                        for ab in range(2):
                            bk = 2 * c + ab
                            for k in range(8):
                                kb.op("pe", lambda e, k=k, ab=ab, c=c, bk=bk: e.matmul(
                                    ps[:, bk, 0:w], wu[:, slot, k, ab, c * 128:(c + 1) * 128], xn[:, k, t0:t0 + w],
                                    start=(k == 0), stop=(k == 7)),
                                    reads=[wu.buf((slot, ab)), xn.buf((k, bi))], writes=[bank(bk)], inc=(k == 7))
                                if k == 3:
                                    yield
                            if ab == 0:
                                kb.op("act", lambda e, c=c: e.activation(sil[:, c, 0:w], ps[:, 2 * c, 0:w], AF.Silu),
                                      reads=[bank(2 * c)], writes=[sil.buf(c)])
                            else:
                                kb.op("dve", lambda e, c=c: e.tensor_tensor(act[:, pp, c, 0:w], sil[:, c, 0:w],
                                                                            ps[:, 2 * c + 1, 0:w], ALU.mult),
                                      reads=[sil.buf(c), bank(2 * c + 1)], writes=[act.buf((pp, c))])
                            yield

                def down_tile(i, ds):
                    gi, bi = steps[i]
                    slot = gi % 2
                    t0, w = BLKS[bi]
                    pp = i % 2
                    bk = 4 + ds % 3
                    for c in range(2):
                        kb.op("pe", lambda e, c=c: e.matmul(
                            ps[:, bk, 0:w], wd[:, slot, c, ds * 128:(ds + 1) * 128], act[:, pp, c, 0:w],
                            start=(c == 0), stop=(c == 1)),
                            reads=[wd.buf(slot), act.buf((pp, c))], writes=[bank(bk)], inc=(c == 1))
                    kb.op("dve", lambda e: e.scalar_tensor_tensor(
                        h[:, ds, t0:t0 + w], ps[:, bk, 0:w], mscal(bi, 2, n, ds), h[:, ds, t0:t0 + w], ALU.mult, ALU.add),
                        reads=[bank(bk), hb(ds, bi), mder.buf()], writes=[hb(ds, bi)])

                def emit_up(i):
                    for _ in up_gen(i):
                        pass

                if (l, which) not in prefetched:
                    load_group(0)
                    load_group(1)
                mp = 0
                mbase = (0 if which == 1 else 18)
                if next_mod_layer is not None:
                    mod_dma(next_mod_layer, mbase + 0)
                    mod_dma(next_mod_layer, mbase + 1)
                emit_up(0)
                for i in range(len(steps)):
                    gi, bi = steps[i]
                    ug = up_gen(i + 1) if i + 1 < len(steps) else iter(())
                    next(ug, None)
                    next(ug, None)
                    for ds in range(8):
                        down_tile(i, ds)
                        next(ug, None)
                    for _ in ug:
                        pass
                    if bi == n_blocks - 1:
                        if gi + 2 < 11:
                            load_group(gi + 2)
                        if next_mod_layer is not None:
                            for _ in range(2):
                                if mp < 18:
                                    mod_mm(next_mod_layer, mbase + mp)
                                    if mp + 2 < 18:
                                        mod_dma(next_mod_layer, mbase + mp + 2)
                                    mp += 1
                if next_mod_layer is not None:
                    while mp < 18:
                        mod_mm(next_mod_layer, mbase + mp)
                        if mp + 2 < 18:
                            mod_dma(next_mod_layer, mbase + mp + 2)
                        mp += 1
                nxt = None if which == 1 else ((l + 1, 1) if l + 1 < n_layers else None)
                if nxt is not None and stop_after is None:
                    nup = (f1u_d if nxt[1] == 1 else f2u_d)[nxt[0]].rearrange("(k p) n -> p k n", p=128)
                    ndn = (f1d_d if nxt[1] == 1 else f2d_d)[nxt[0]].rearrange("(c p) n -> p c n", p=128)
                    load_group(0, nup, ndn)
                    load_group(1, nup, ndn)
                    prefetched.add(nxt)
                kb.barrier()

        def attend(qT, q0, w, steps, kT, vaug, scale, ostage, out_dram, out_buf, obase, mask_fn, P, Pbuf_i, tick=None, pending=None):
            base = Pbuf_i[0]
            ob = obase
            ns = len(steps)

            def s_step(si):
                kt = steps[si][0]
                sb_ = ((base + si) % 2) * 2
                for hh in range(2):
                    pr = slice(64 * hh, 64 * hh + 64)
                    kb.op("pe", lambda e, hh=hh, pr=pr: e.matmul(ps[:, sb_ + hh, 0:w], kT[pr, kt * 128:(kt + 1) * 128],
                                                                 qT[pr, q0:q0 + w], start=True, stop=True),
                          reads=[kT.buf(), qT.buf()], writes=[bank(sb_ + hh)], inc=True)

            s_step(0)
            for si, (kt, mi) in enumerate(steps):
                if si + 1 < ns:
                    s_step(si + 1)
                sb_ = ((base + si) % 2) * 2
                pp = (base + si) % 2
                kb.op("act", lambda e: e.activation(P[:, pp, :, 0:w], ps[:, sb_:sb_ + 2, 0:w], AF.Exp, scale=scale),
                      reads=[bank(sb_), bank(sb_ + 1)], writes=[P.buf(pp)])
                if mi is not None:
                    mask_fn(mi, P, pp, w)
                if tick is not None:
                    tick(sb_)
                if pending is not None and pending and si == min(2, ns - 1):
                    pending.pop(0)()
                for hh in range(2):
                    kb.op("pe", lambda e, hh=hh: e.matmul(ps[:, ob + hh, 0:w], vaug[:, kt, hh, :], P[:, pp, hh, 0:w],
                                                          start=(si == 0), stop=(si == ns - 1)),
                          reads=[vaug.buf(), P.buf(pp)], writes=[bank(ob + hh)], inc=(si == ns - 1))
            sidx = base + ns
            Pbuf_i[0] = sidx

            def fin():
                for hh in range(2):
                    o_pr = slice(64 * hh, 64 * hh + 64)
                    d_pr = slice(64 * (1 - hh), 64 * (1 - hh) + 64)
                    rd = ostage["rd"]
                    kb.op("dve", lambda e, hh=hh, d_pr=d_pr: e.reciprocal(rd[d_pr, hh, 0:w], ps[d_pr, ob + hh, 0:w]),
                          reads=[bank(ob + hh)], writes=[rd.buf(hh)])
                    os_ = ostage["o"]
                    oi = ostage["i"]
                    kb.op("dve", lambda e, hh=hh, o_pr=o_pr, d_pr=d_pr: e.tensor_tensor(
                        os_[o_pr, oi, 0:w], ps[o_pr, ob + hh, 0:w], rd[d_pr, hh, 0:w], ALU.mult),
                        reads=[bank(ob + hh), rd.buf(hh)], writes=[os_.buf((oi, hh))])
                oi = ostage["i"]
                kb.dma("sp", out_dram[:, q0:q0 + w], ostage["o"][:, oi, 0:w],
                       reads=[ostage["o"].buf((oi, 0)), ostage["o"].buf((oi, 1))], writes=[out_buf])
                ostage["i"] = (oi + 1) % 2

            if pending is None:
                fin()
            else:
                pending.append(fin)

        def mixer_phase(l, last):
            n_blocks = 5
            nqb = 4 if last else 5
            with ExitStack() as ph:
                rstd = stats(ph, n_blocks)
                win = sbt(ph, "win", [128, 8, 1792], BF16)
                cosT = sbt(ph, "cosT", [128, 512], F32)
                sinT = sbt(ph, "sinT", [128, 512], F32)
                aT = sbt(ph, "aT", [128, 8, 512], BF16)
                tmp = sbt(ph, "ntmp", [128, 2, 512], F32)
                stg = sbt(ph, "stg", [128, 2, 512], F32)
                stgb = sbt(ph, "stgb", [128, 4, 512], BF16)
                vst = sbt(ph, "vst", [128, 2, 6, 128], BF16)
                sqh = sbt(ph, "sqh", [128, 2, 512], BF16)
                rr = sbt(ph, "rr", [128, 2, 512], F32)
                qn = sbt(ph, "qn", [128, 2, 512], F32)
                qg = sbt(ph, "qg", [128, 2, 512], F32)
                sgc = [0, 0]
                vic = [0]
                wsrc = win_d[l].rearrange("(k p) n -> p k n", p=128)
                for pc in range(4):
                    kb.dma("pool", win[:, :, pc * 448:(pc + 1) * 448], wsrc[:, :, pc * 448:(pc + 1) * 448],
                           writes=[win.buf(pc)])
                winb = [win.buf(pc) for pc in range(4)]
                if stop_after is None:
                    prefetch_ffn(l, 2)
                kb.op("dve", lambda e: e.memset(vst[:], 1.0), writes=[vst.buf(0), vst.buf(1)])
                for bi in range(n_blocks):
                    t0, w = BLKS[bi]
                    kb.dma("sp", cosT[:, 0:w], cos_d[:, t0:t0 + w], writes=[cosT.buf()])
                    kb.dma("sp", sinT[:, 0:w], sin_d[:, t0:t0 + w], writes=[sinT.buf()])
                    norm_block(bi, 1, rstd, lambda s: aT[:, s, 0:w], lambda s: aT.buf(s), tmp)
                    aTb = [aT.buf(s) for s in range(8)]
                    simple_i = [0]

                    def P(si):
                        qk = si >= 6
                        par = (si - 6) % 2
                        if qk:
                            bk = par
                        else:
                            bk = 4 + simple_i[0] % 2
                            simple_i[0] += 1
                        for k in range(8):
                            kb.op("pe", lambda e, k=k: e.matmul(
                                ps[:, bk, 0:w], win[:, k, si * 128:(si + 1) * 128], aT[:, k, 0:w],
                                start=(k == 0), stop=(k == 7)),
                                reads=winb + aTb, writes=[bank(bk)], inc=(k == 7))
                        if si < 4:
                            sg = sgc[0] % 4
                            sgc[0] += 1
                            kb.op("act", lambda e: e.activation(stgb[:, sg, 0:w], ps[:, bk, 0:w], AF.Copy),
                                  reads=[bank(bk)], writes=[stgb.buf(sg)])
                            kb.dma("sp", QKn[si, :, t0:t0 + w], stgb[:, sg, 0:w], reads=[stgb.buf(sg)], writes=[QKn.buf(si)])
                        elif si < 6:
                            sg = sgc[1] % 2
                            sgc[1] += 1
                            kb.op("act", lambda e: e.activation(stg[:, sg, 0:w], ps[:, bk, 0:w], AF.Copy),
                                  reads=[bank(bk)], writes=[stg.buf(sg)])
                            kb.dma("sp", Usc[si - 4, :, t0:t0 + w], stg[:, sg, 0:w], reads=[stg.buf(sg)],
                                   writes=[Usc.buf(si - 4)])
                        else:
                            gcol = l * 2 + (0 if si < 10 else 1)
                            kb.op("act", lambda e: e.activation(sqh[:, par, 0:w], ps[:, bk, 0:w], AF.Square),
                                  reads=[bank(bk)], writes=[sqh.buf(par)])
                            kb.op("act", lambda e: e.activation(qg[:, par, 0:w], ps[:, bk, 0:w], AF.Identity,
                                                                scale=qkg[:, gcol:gcol + 1]),
                                  reads=[bank(bk), qkg.buf()], writes=[qg.buf(par)])

                    def H(si):
                        par = (si - 6) % 2
                        bk = 2 if par == 0 else 6
                        kb.op("pe", lambda e: e.matmul(ps[:, bk, 0:w], hm[:], sqh[:, par, 0:w], start=True, stop=True),
                              reads=[hm.buf(), sqh.buf(par)], writes=[bank(bk)])
                        kb.op("act", lambda e: e.activation(rr[:, par, 0:w], ps[:, bk, 0:w], AF.Sqrt, bias=epsc[:, 0:1]),
                              reads=[bank(bk), epsc.buf()], writes=[rr.buf(par)])
                        kb.op("dve", lambda e: e.reciprocal(rr[:, par, 0:w], rr[:, par, 0:w]), reads=[rr.buf(par)],
                              writes=[rr.buf(par)])
                        kb.op("dve", lambda e: e.tensor_tensor(qn[:, par, 0:w], qg[:, par, 0:w], rr[:, par, 0:w], ALU.mult),
                              reads=[qg.buf(par), rr.buf(par)], writes=[qn.buf(par)])

                    def R(si):
                        par = (si - 6) % 2
                        bk = 3 if par == 0 else 7
                        kb.op("pe", lambda e: e.matmul(ps[:, bk, 0:w], rotm[:], qn[:, par, 0:w], start=True, stop=True),
                              reads=[rotm.buf(), qn.buf(par)], writes=[bank(bk)])
                        kb.op("dve", lambda e: e.tensor_tensor(rr[:, par, 0:w], ps[:, bk, 0:w], sinT[:, 0:w], ALU.mult),
                              reads=[bank(bk), sinT.buf()], writes=[rr.buf(par)])
                        kb.op("dve", lambda e: e.tensor_tensor(qn[:, par, 0:w], qn[:, par, 0:w], cosT[:, 0:w], ALU.mult),
                              reads=[qn.buf(par), cosT.buf()], writes=[qn.buf(par)])
                        sg = sgc[0] % 4
                        sgc[0] += 1
                        kb.op("dve", lambda e: e.tensor_tensor(stgb[:, sg, 0:w], qn[:, par, 0:w], rr[:, par, 0:w], ALU.add),
                              reads=[qn.buf(par), rr.buf(par)], writes=[stgb.buf(sg)])
                        if si < 10:
                            kb.dma("sp", QCs[si - 6, :, t0:t0 + w], stgb[:, sg, 0:w], reads=[stgb.buf(sg)],
                                   writes=[QCs.buf(si - 6)])
                        else:
                            kb.dma("sp", KCs[:, t0:t0 + w], stgb[:, sg, 0:w], reads=[stgb.buf(sg)], writes=[KCs.buf()])

                    def V(tl_):
                        tt = t0 // 128 + tl_
                        bk = 4 + simple_i[0] % 2
                        simple_i[0] += 1
                        for k in range(8):
                            kb.op("pe", lambda e, k=k: e.matmul(
                                ps[:, bk, 0:384], aT[:, k, tl_ * 128:(tl_ + 1) * 128], win[:, k, 1408:1792],
                                start=(k == 0), stop=(k == 7)),
                                reads=winb + aTb, writes=[bank(bk)], inc=(k == 7))
                        vs = vic[0] % 2
                        vic[0] += 1
                        pv = ps[:, bk, 0:384].rearrange("p (a b c) -> p a b c", a=3, b=2)
                        vv = vst[:, vs].rearrange("p (a b) c -> p a b c", b=2)
                        kb.op("act", lambda e: e.activation(vv[:, :, 0, 0:64], pv[:, :, 0, :], AF.Copy),
                              reads=[bank(bk)], writes=[vst.buf(vs)])
                        kb.op("act", lambda e: e.activation(vv[:, :, 1, 64:128], pv[:, :, 1, :], AF.Copy),
                              reads=[bank(bk)], writes=[vst.buf(vs)])
                        kb.dma("sp", VAs[tt], vst[:, vs], reads=[vst.buf(vs)], writes=[VAs.buf()])

                    nv = w // 128
                    tasks = [(P, 6), (P, 0), (H, 6), (P, 7), (P, 1), (H, 7), (P, 8), (R, 6), (P, 2), (H, 8), (P, 9), (R, 7),
                             (P, 3), (H, 9), (P, 10), (R, 8), (P, 4), (H, 10), (P, 5), (R, 9), (V, 0), (R, 10)]
                    tasks += [(V, i) for i in range(1, nv)]
                    for fn_, arg in tasks:
                        fn_(arg)
                kb.barrier()

            with ExitStack() as ph:
                kT2 = [sbt(ph, "kT%d" % i, [128, T], BF16) for i in range(2)]
                qT2n = [sbt(ph, "qT%d" % i, [128, T], BF16) for i in range(2)]
                vaug2 = [sbt(ph, "vaug%d" % i, [128, 18, 2, 128], BF16) for i in range(2)]
                brw2 = [sbt(ph, "brw%d" % i, [128, 2, 15, 64], F32) for i in range(2)]
                ECt2 = [sbt(ph, "ECt%d" % i, [128, 2, 22, 64], BF16) for i in range(2)]
                Tt2 = [sbt(ph, "Tt%d" % i, [128, 2, 22, 64], BF16) for i in range(2)]
                cur = [0]
                pend = []
                P = sbt(ph, "P", [128, 2, 2, 512], BF16)
                rd = sbt(ph, "rd", [128, 2, 512], F32)
                osg = sbt(ph, "osg", [128, 2, 512], BF16)
                ostage = {"rd": rd, "o": osg, "i": 0}
                Pi = [0]
                obank = [0]

                def na_mask(mi, P_, pp, w):
                    kind, s0, ri = mi
                    tab = Tt2[cur[0]] if kind == "T" else ECt2[cur[0]]
                    pv = P_[:, pp].rearrange("p a (b c) -> p a b c", b=8)
                    kb.op("dve", lambda e: e.tensor_tensor(pv, pv, tab[:, :, s0:s0 + 8, :], ALU.mult),
                          reads=[P_.buf(pp), tab.buf()], writes=[P_.buf(pp)])
                    if kind == "E":
                        rmv = rmask[:, ri, :].unsqueeze(1).unsqueeze(3).broadcast_to([128, 2, 8, 64])
                        kb.op("dve", lambda e: e.tensor_tensor(pv, pv, rmv, ALU.mult),
                              reads=[P_.buf(pp), rmask.buf()], writes=[P_.buf(pp)])

                for s in range(2):
                    kT, qT, vaug, brw, ECt, Tt = kT2[s], qT2n[s], vaug2[s], brw2[s], ECt2[s], Tt2[s]
                    for hh in range(2):
                        for half in range(2):
                            kb.dma("sp", brw[64 * half:64 * half + 64, hh], rpbx_d[l, 2 * s + hh], writes=[brw.buf(hh)])
                    kb.dma("sp", kT[:], QKn[2 + s], reads=[QKn.buf(2 + s)], writes=[kT.buf()])
                    kb.dma("sp", qT[:], QKn[s], reads=[QKn.buf(s)], writes=[qT.buf()])
                    kb.dma("sp", vaug[:], VAs[:, :, 2 * s:2 * s + 2, :].rearrange("t p s c -> p t s c"),
                           reads=[VAs.buf()], writes=[vaug.buf()])
                    kb.op("dve", lambda e: e.memset(ECt[:], 0.0), writes=[ECt.buf()])
                    kb.op("dve", lambda e: e.memset(Tt[:], 0.0), writes=[Tt.buf()])
                    for hh in range(2):
                        for half in range(2):
                            pr = slice(64 * half, 64 * half + 64)
                            kb.op("act", lambda e, hh=hh, pr=pr, half=half: e.activation(
                                ECt[pr, hh, 3 + half:18 + half, :], brw[pr, hh, :, :], AF.Exp),
                                reads=[brw.buf(hh)], writes=[ECt.buf()])
                            kb.op("act", lambda e, hh=hh, pr=pr, half=half: e.activation(
                                Tt[pr, hh, 7 + half:15 + half, :], brw[pr, hh, 4:12, :], AF.Exp),
                                reads=[brw.buf(hh)], writes=[Tt.buf()])
                for s in range(2):
                    kT, qT, vaug = kT2[s], qT2n[s], vaug2[s]
                    cur[0] = s
                    for qb in range(nqb):
                        q0, w = BLKS[qb]
                        if qb < 4:
                            R = 8 * qb
                            krs = {0: range(0, 12, 2), 8: range(4, 20, 2), 16: range(12, 28, 2), 24: range(20, 32, 2)}[R]
                            steps = []
                            for ti, kr in enumerate(krs):
                                s0 = 10 - (kr - R)
                                if R in (8, 16):
                                    steps.append((kr // 2, ("T", s0, 0)))
                                else:
                                    steps.append((kr // 2, ("E", s0, (0 if R == 0 else 6) + ti)))
                            steps += [(16, None), (17, None)]
                        else:
                            steps = [(16, None), (17, None)]
                        ob = 4 + 2 * (obank[0] % 2)
                        obank[0] += 1
                        attend(qT, q0, w, steps, kT, vaug, 0.125, ostage, OTs[s], OTs.buf(s), ob, na_mask, P, Pi, pending=pend)
                while pend:
                    pend.pop(0)()
                kb.barrier()

            ph_wo = ExitStack()
            wo = sbt(ph_wo, "wo", [128, 8, D], BF16)
            kb.dma("pool", wo[:], wout_d[l].rearrange("(k p) n -> p k n", p=128), writes=[wo.buf()])

            with ExitStack() as ph:
                kT = sbt(ph, "kTc", [128, T], BF16)
                qT2 = [sbt(ph, "qTc%d" % i, [128, T], BF16) for i in range(2)]
                vaug = sbt(ph, "vaugc", [128, 18, 2, 128], BF16)
                P = sbt(ph, "Pc", [128, 2, 2, 512], BF16)
                rd = sbt(ph, "rdc", [128, 2, 512], F32)
                osg = sbt(ph, "osgc", [128, 2, 512], BF16)
                ostage = {"rd": rd, "o": osg, "i": 0}
                Pi = [0]
                obank = [0]
                pool_ops = []
                pend = []

                def rec_op(*a_, **k_):
                    pool_ops.append((kb.op, a_, k_))

                def rec_dma(*a_, **k_):
                    pool_ops.append((kb.dma, a_, k_))

                def tick(bk_=0, nmax=1):
                    for _ in range(nmax):
                        if pool_ops:
                            f_, a_, k_ = pool_ops.pop(0)
                            if f_ in (kb.op, kb.dma):
                                f_(*a_, **k_)
                            else:
                                f_(bk=bk_)

                PADN = NXT + 32
                X0 = sbt(ph, "pX0", [128, PADN], F32)
                X1 = sbt(ph, "pX1", [128, PADN], F32)
                X2 = sbt(ph, "pX2", [128, PADN], F32)
                yb = sbt(ph, "pyb", [128, NXT], BF16)
                et = sbt(ph, "pet", [128, 2, 8], F32)
                wbd = sbt(ph, "wbd", [128, 2, 128], BF16)
                ostg = sbt(ph, "postg", [128, 2, 512], BF16)
                rec_op("dve", lambda e: e.memset(wbd[:], 0.0), writes=[wbd.buf()])
                for s in range(2):
                    for g in range(2):
                        rec_dma("pool", wbd[64 * g:64 * g + 64, s, 64 * g:64 * g + 64], poolw_d[l, 2 * s + g], writes=[wbd.buf()])
                oic = [0]

                def pool_seg(s, seg):
                    if True:
                        n = NXT if seg == 0 else NCT
                        tb = 0 if seg == 0 else NXT
                        bufs = [X0.buf(), X1.buf(), X2.buf()]
                        rec_op("dve", lambda e: e.memset(X0[:, 0:16], 0.0), writes=[X0.buf()])
                        rec_op("dve", lambda e, n=n: e.memset(X0[:, 16 + n:32 + n], 0.0), writes=[X0.buf()])
                        rec_dma("sp", X0[:, 16:16 + n], Usc[s, :, tb:tb + n], reads=[Usc.buf(s)], writes=[X0.buf()])

                        def sa(dst, src, lo, hi, d1, d2, pr=slice(0, 128)):
                            rec_op("dve", lambda e: e.tensor_tensor(dst[pr, lo + 16:hi + 16], src[pr, lo + 16 + d1:hi + 16 + d1],
                                                                   src[pr, lo + 16 + d2:hi + 16 + d2], ALU.add),
                                  reads=[src.buf()], writes=[dst.buf()])

                        pa, pb = slice(0, 64), slice(64, 128)
                        sa(X1, X0, -14, n + 14, -1, 0)
                        if s == 0:
                            rec_op("dve", lambda e: e.tensor_copy(X2[pa, 16:16 + n], X1[pa, 16:16 + n]), reads=[X1.buf()],
                                  writes=[X2.buf()])
                            sa(X2, X1, 0, n, -1, 1, pb)
                            ws = (2, 4)
                        else:
                            sa(X2, X1, -12, n + 12, -1, 1)
                            sa(X1, X2, 0, n, -2, 2, pa)
                            sa(X1, X2, -8, n + 8, -2, 2, pb)
                            sa(X2, X1, 0, n, -4, 4, pb)
                            rec_op("dve", lambda e: e.tensor_copy(X2[pa, 16:16 + n], X1[pa, 16:16 + n]), reads=[X1.buf()],
                                  writes=[X2.buf()])
                            ws = (8, 16)
                        for g, pr in enumerate((pa, pb)):
                            rec_op("dve", lambda e, g=g, pr=pr: e.scalar_tensor_tensor(
                                yb[pr, 0:n], X2[pr, 16:16 + n], 1.0 / ws[g], X0[pr, 16:16 + n], ALU.mult, ALU.subtract),
                                reads=[X2.buf(), X0.buf()], writes=[yb.buf()])
                        for ed in range(2):
                            e0 = 0 if ed == 0 else n - 8
                            rec_op("dve", lambda e, ed=ed, e0=e0: e.tensor_tensor(
                                et[:, ed, :], X2[:, 16 + e0:24 + e0], prc[:, s, seg, ed * 8:ed * 8 + 8], ALU.mult),
                                reads=[X2.buf(), prc.buf()], writes=[et.buf(ed)])
                            rec_op("dve", lambda e, ed=ed, e0=e0: e.tensor_tensor(
                                yb[:, e0:e0 + 8], et[:, ed, :], X0[:, 16 + e0:24 + e0], ALU.subtract),
                                reads=[et.buf(ed), X0.buf()], writes=[yb.buf()])
                        for c0 in range(0, n, 512):
                            w = min(512, n - c0)
                            bk = 2 * ((c0 // 512) % 2)
                            o_ = oic[0] % 2
                            oic[0] += 1

                            def bundle(bk=0, c0=c0, w=w, o_=o_):
                                kb.op("pe", lambda e: e.matmul(ps[:, bk, 0:w], wbd[:, s, :], yb[:, c0:c0 + w],
                                                               start=True, stop=True),
                                      reads=[wbd.buf(), yb.buf()], writes=[bank(bk)])
                                kb.op("dve", lambda e: e.tensor_scalar(ostg[:, o_, 0:w], ps[:, bk, 0:w],
                                                                       pscale[:, l * 2 + s:l * 2 + s + 1], None, ALU.mult),
                                      reads=[bank(bk), pscale.buf()], writes=[ostg.buf(o_)])
                                kb.dma("sp", OTs[2 + s, :, tb + c0:tb + c0 + w], ostg[:, o_, 0:w], reads=[ostg.buf(o_)],
                                       writes=[OTs.buf(2 + s)])

                            pool_ops.append((bundle, (), {}))

                for s_ in range(2):
                    for seg_ in range(1 if last else 2):
                        pool_seg(s_, seg_)
                kb.dma("sp", kT[:], KCs[:], reads=[KCs.buf()], writes=[kT.buf()])
                kb.dma("sp", vaug[:], VAs[:, :, 4:6, :].rearrange("t p s c -> p t s c"), reads=[VAs.buf()],
                       writes=[vaug.buf()])
                kb.dma("sp", qT2[0][:], QCs[0], reads=[QCs.buf(0)], writes=[qT2[0].buf()])
                for j in range(4):
                    qT = qT2[j % 2]
                    if j + 1 < 4:
                        kb.dma("sp", qT2[(j + 1) % 2][:], QCs[j + 1], reads=[QCs.buf(j + 1)], writes=[qT2[(j + 1) % 2].buf()])
                    for qb in range(nqb):
                        q0, w = BLKS[qb]
                        steps = [(kt, None) for kt in (range(18) if qb < 4 else (16, 17))]
                        ob = 4 + 2 * (obank[0] % 2)
                        obank[0] += 1
                        attend(qT, q0, w, steps, kT, vaug, 0.125, ostage, OTs[4 + j], OTs.buf(4 + j), ob, None, P, Pi, tick=tick, pending=pend)
                while pend:
                    pend.pop(0)()
                tick(0, 100000)
                kb.barrier()

            with ExitStack() as ph:
                oT = sbt(ph, "oTb", [128, 2, 8, 512], BF16)
                otb = [OTs.buf(i) for i in range(8)]
                for bi in range(nqb):
                    t0, w = BLKS[bi]
                    sl = bi % 2
                    kb.dma("sp", oT[:, sl, :, 0:w], OTs[:, :, t0:t0 + w].rearrange("s p t -> p s t"), reads=otb,
                           writes=[oT.buf(sl)])
                    for ds in range(8):
                        bk = ds % 4
                        for k in range(8):
                            kb.op("pe", lambda e, k=k, ds=ds, bk=bk: e.matmul(
                                ps[:, bk, 0:w], wo[:, k, ds * 128:(ds + 1) * 128], oT[:, sl, k, 0:w],
                                start=(k == 0), stop=(k == 7)),
                                reads=[wo.buf(), oT.buf(sl)], writes=[bank(bk)], inc=(k == 7))
                        kb.op("dve", lambda e, ds=ds, bk=bk: e.scalar_tensor_tensor(
                            h[:, ds, t0:t0 + w], ps[:, bk, 0:w], mscal(bi, 2, 1, ds), h[:, ds, t0:t0 + w], ALU.mult, ALU.add),
                            reads=[bank(bk), hb(ds, bi), mder.buf()], writes=[hb(ds, bi)])
                kb.barrier()
            ph_wo.close()

        for l in range(n_layers):
            last = (l == L - 1)
            nm = l + 1 if l + 1 < n_layers else None
            ffn_phase(l, 1, 5, nm)
            if stop_after == (l, 1):
                break
            mixer_phase(l, last)
            if stop_after == (l, 2):
                break
            ffn_phase(l, 2, 4 if last else 5, nm)
            if nm is not None:
                finalize_mod(nm)
                kb.barrier()

        with ExitStack() as ph:
            if dbg:
                kb.dma("sp", hd_d, h[:], reads=[hb(s, bi) for s in range(8) for bi in range(5)])
            rstd = stats(ph, 4)
            yf = sbt(ph, "yf", [128, 2, 8, 512], F32)
            ot = sbt(ph, "ot", [128, 2, D], F32)
            oi = 0
            for bi in range(4):
                t0, w = BLKS[bi]
                sl = bi % 2
                for s in range(8):
                    kb.op("dve", lambda e, s=s: e.scalar_tensor_tensor(
                        yf[:, sl, s, :], h[:, s, t0:t0 + w], fg[:, s:s + 1], rstd[:, t0:t0 + w], ALU.mult, ALU.mult),
                        reads=[hb(s, bi), fg.buf(), rstd.buf(bi)], writes=[yf.buf((sl, s))])
                for tl_ in range(4):
                    o_ = oi % 2
                    oi += 1
                    for hf in range(2):
                        bk = (oi * 2 + hf) % 6
                        for s4 in range(4):
                            s = hf * 4 + s4
                            kb.op("pe", lambda e, s=s, s4=s4, bk=bk: e.transpose(
                                ps[:, bk, s4 * 128:(s4 + 1) * 128], yf[:, sl, s, tl_ * 128:(tl_ + 1) * 128], ident[:]),
                                reads=[yf.buf((sl, s)), ident.buf()], writes=[bank(bk)], inc=(s4 == 3))
                        if hf == 0:
                            kb.op("dve", lambda e, bk=bk, o_=o_: e.tensor_copy(ot[:, o_, 0:512], ps[:, bk, :]),
                                  reads=[bank(bk)], writes=[ot.buf((o_, 0))])
                        else:
                            kb.op("act", lambda e, bk=bk, o_=o_: e.activation(ot[:, o_, 512:1024], ps[:, bk, :], AF.Copy),
                                  reads=[bank(bk)], writes=[ot.buf((o_, 1))])
                    r0 = t0 + tl_ * 128
                    kb.dma("sp", out_d[r0:r0 + 128, :], ot[:, o_, :], reads=[ot.buf((o_, 0)), ot.buf((o_, 1))])
            kb.barrier()
    return nc


def _consts():
    ident = np.eye(128, dtype=np.float32)
    rotm = np.zeros((128, 128), np.float32)
    for m in range(128):
        d = m % 64
        q16 = d % 32
        partner = m + 16 if q16 < 16 else m - 16
        rotm[partner, m] = 1.0
    hm = np.zeros((128, 128), np.float32)
    hm[0:64, 0:64] = 1.0 / 64
    hm[64:128, 64:128] = 1.0 / 64
    inv_freq = (10000.0 ** (-np.arange(0, 32, 2, dtype=np.float32) / 32.0)).astype(np.float32)
    pos = np.arange(NXT)
    ang_row = (pos // 64).astype(np.float32)[:, None] * inv_freq[None, :]
    ang_col = (pos % 64).astype(np.float32)[:, None] * inv_freq[None, :]
    cosT = np.ones((128, T), np.float32)
    sinT = np.zeros((128, T), np.float32)
    for p in range(128):
        d = p % 64
        ang = ang_row if d < 32 else ang_col
        j = d % 16
        first = (d % 32) < 16
        cosT[p, :NXT] = np.cos(ang[:, j])
        sinT[p, :NXT] = (-1.0 if first else 1.0) * np.sin(ang[:, j])
    rmask = np.zeros((128, 12, 8), np.float32)
    for ci, (R, krs) in enumerate(((0, range(0, 12, 2)), (24, range(20, 32, 2)))):
        for ti, kr in enumerate(krs):
            for kl in range(2):
                for qr in range(8):
                    r = R + qr
                    r0 = min(max(r - 4, 0), 24)
                    if r0 <= kr + kl < r0 + 8:
                        rmask[64 * kl:64 * kl + 64, ci * 6 + ti, qr] = 1.0
    pcnt = np.ones((128, 2, 2, 16), np.float32)
    for s in range(2):
        for g in range(2):
            wdw = (2, 4, 8, 16)[2 * s + g]
            for seg, n in enumerate((NXT, NCT)):
                for ed in range(2):
                    for i in range(8):
                        t = i if ed == 0 else n - 8 + i
                        lo = max(t - wdw // 2, 0)
                        hi = min(t - wdw // 2 + wdw, n)
                        pcnt[64 * g:64 * g + 64, s, seg, ed * 8 + i] = hi - lo
    return dict(ident=ident, rotm=rotm, hm=hm, cosT=cosT, sinT=sinT,
                rmask=rmask.reshape(128, 96), pcnt=pcnt.reshape(128, 64))


def _rpb_expand(na_rpb):
    qc = np.arange(64)
    kc = np.arange(64)
    win_c0 = np.clip(qc - 8, 0, 48)
    valid = (kc[:, None] >= win_c0[None, :]) & (kc[:, None] < win_c0[None, :] + 16)
    dx = np.clip(kc[:, None] - qc[None, :], -15, 15) + 15
    g = na_rpb[:, :, ::-1, :][:, :, :, dx]
    g = np.where(valid[None, None, None], g, np.float32(NEGB)).astype(np.float32)
    return np.ascontiguousarray(g.transpose(0, 1, 3, 2, 4))


def _prep_shared(inp):
    f = lambda a: np.ascontiguousarray(np.asarray(a, dtype=np.float32))
    qperm = np.concatenate([np.arange(64) + 64 * hh for hh in (0, 4, 1, 5, 2, 6, 3, 7)])
    w_in = f(inp["w_in"])
    cols = np.concatenate([np.arange(0, 256), np.arange(256, 512), np.arange(768, 1024), 1024 + qperm,
                           np.arange(1536, 1664), np.arange(512, 768), np.arange(1664, 1792)])
    w_in_p = np.ascontiguousarray(w_in[:, :, cols])
    w_out = f(inp["w_out"])
    rows = np.concatenate([np.arange(0, 512), 512 + qperm])
    w_out_p = np.ascontiguousarray(w_out[:, rows, :])
    bada = f(inp["b_ada"]).reshape(L, 72, 128).transpose(2, 0, 1).reshape(128, L * 72)
    normg = f(inp["norm_g"]).reshape(L, 3, 8, 128).transpose(3, 0, 1, 2).reshape(128, L * 24)
    pscale = f(inp["pool_scale"]).reshape(L, 2, 128).transpose(2, 0, 1).reshape(128, L * 2)
    qg = np.tile(f(inp["q_norm_g"]), (1, 2))
    kg = np.tile(f(inp["k_norm_g"]), (1, 2))
    qkg = np.stack([qg, kg], axis=-1).transpose(1, 0, 2).reshape(128, L * 2)
    fg = f(inp["final_g"]).reshape(8, 128).T
    sh = dict(w_ada=f(inp["w_ada"]), bada=np.ascontiguousarray(bada), normg=np.ascontiguousarray(normg),
              ffn1_up=f(inp["ffn1_up"]), ffn1_down=f(inp["ffn1_down"]), ffn2_up=f(inp["ffn2_up"]),
              ffn2_down=f(inp["ffn2_down"]), w_in=w_in_p, w_out=w_out_p, rpbx=_rpb_expand(f(inp["na_rpb"])),
              pool_w=f(inp["pool_w"]), pscale=np.ascontiguousarray(pscale), qkg=np.ascontiguousarray(qkg),
              fg=np.ascontiguousarray(fg))
    sh.update(_consts())
    return sh


def _in_maps(inp, n=8):
    sh = _prep_shared(inp)
    x = np.asarray(inp["x"], np.float32)
    ctx = np.asarray(inp["ctx"], np.float32)
    c = np.asarray(inp["c"], np.float32)
    c_ctx = np.asarray(inp["c_ctx"], np.float32)
    maps = []
    for b in range(n):
        m = dict(sh)
        m["xin"] = np.ascontiguousarray(np.concatenate([x[b], ctx[b]], axis=0))
        m["cc"] = np.ascontiguousarray(np.stack([c[b].reshape(8, 128).T, c_ctx.reshape(8, 128).T], axis=-1))
        maps.append(m)
    return maps


def kernel(**inputs):
    nc = build_program()
    maps = _in_maps(inputs, 8)
    res = run_bass_kernel_spmd(nc, maps, core_ids=list(range(8)))
    return np.stack([np.asarray(r["out"], dtype=np.float32) for r in res.results], axis=0)
```

```python
import math
from contextlib import ExitStack

import numpy as np
import concourse.bass as bass
import concourse.mybir as mybir
from concourse.bass_utils import run_bass_kernel_spmd

F32 = mybir.dt.float32
BF16 = mybir.dt.bfloat16
ALU = mybir.AluOpType
AF = mybir.ActivationFunctionType

D = 1024
T = 2304
NXT = 2048
NCT = 256
L = 4
DFF = 2816
EPS = 1e-6
BLKS = [(0, 512), (512, 512), (1024, 512), (1536, 512), (2048, 256)]
NEGB = -30000.0


class Sem:
    def __init__(self, nc, es, name):
        self.sem = es.enter_context(nc.semaphore(name))
        self.count = 0


class Buf:
    __slots__ = ("w", "r")

    def __init__(self):
        self.w = None
        self.r = {}


class KB:
    def __init__(self, nc, es):
        self.nc = nc
        self.eng = {"pe": nc.tensor, "act": nc.scalar, "dve": nc.vector, "pool": nc.gpsimd, "sp": nc.sync}
        self.S = {n: Sem(nc, es, "s_" + n) for n in ["pe", "act", "dve", "pool"]}
        self.seen = {n: {} for n in self.eng}
        self.chans = {q: [Sem(nc, es, "c_%s%d" % (q, i)) for i in range(8)] for q in ["sp", "pool"]}
        self.chan_i = {"sp": 0, "pool": 0}

    def _wait(self, en, deps):
        for S, c in deps.items():
            if en == "pe" and S is self.S["pe"]:
                continue
            if self.seen[en].get(S, 0) < c:
                assert c <= S.count, "wait on future count"
                self.eng[en].wait_ge(S.sem, c)
                self.seen[en][S] = c

    @staticmethod
    def _deps(reads, writes):
        d = {}
        for b in reads:
            if b.w is not None and d.get(b.w[0], 0) < b.w[1]:
                d[b.w[0]] = b.w[1]
        for b in writes:
            if b.w is not None and d.get(b.w[0], 0) < b.w[1]:
                d[b.w[0]] = b.w[1]
            for S, c in b.r.items():
                if d.get(S, 0) < c:
                    d[S] = c
        return d

    def op(self, en, fn, reads=(), writes=(), inc=True):
        self._wait(en, self._deps(reads, writes))
        ins = fn(self.eng[en])
        S = self.S[en]
        if inc:
            S.count += 1
            ins.then_inc(S.sem, 1)
            st = (S, S.count)
        else:
            st = (S, S.count + 1)
        for b in reads:
            if b.r.get(S, 0) < st[1]:
                b.r[S] = st[1]
        for b in writes:
            b.w = st
            b.r = {}
        return ins

    def dma(self, q, out, in_, reads=(), writes=()):
        chs = self.chans[q]
        ch = chs[self.chan_i[q] % len(chs)]
        self.chan_i[q] += 1
        d = self._deps(reads, writes)
        if ch.count > 0 and d.get(ch, 0) < ch.count:
            d[ch] = ch.count
        self._wait(q, d)
        ins = self.eng[q].dma_start(out=out, in_=in_)
        ch.count += 16
        ins.then_inc(ch.sem, 16)
        for b in reads:
            if b.r.get(ch, 0) < ch.count:
                b.r[ch] = ch.count
        for b in writes:
            b.w = (ch, ch.count)
            b.r = {}

    def barrier(self):
        allS = list(self.S.values()) + self.chans["sp"] + self.chans["pool"]
        for en in self.eng:
            self._wait(en, {S: S.count for S in allS if S.count > 0})


class Tl:
    def __init__(self, t):
        self.t = t
        self.b = {}

    def __getitem__(self, k):
        return self.t[k]

    def buf(self, k=0):
        if k not in self.b:
            self.b[k] = Buf()
        return self.b[k]


def build_program(n_layers=L, dbg=False, stop_after=None):
    nc = bass.Bass("TRN2", target_bir_lowering=False)

    def din(name, shape, dt=F32):
        return nc.dram_tensor(name, shape, dt, kind="ExternalInput").ap()

    xin = din("xin", [T, D])
    cc_d = din("cc", [128, 8, 2])
    wada_d = din("w_ada", [L, D, 9 * D])
    bada_d = din("bada", [128, L * 72])
    normg_d = din("normg", [128, L * 24])
    f1u_d = din("ffn1_up", [L, D, 2 * DFF])
    f1d_d = din("ffn1_down", [L, DFF, D])
    f2u_d = din("ffn2_up", [L, D, 2 * DFF])
    f2d_d = din("ffn2_down", [L, DFF, D])
    win_d = din("w_in", [L, D, 1792])
    wout_d = din("w_out", [L, D, D])
    rpbx_d = din("rpbx", [L, 4, 64, 15, 64])
    poolw_d = din("pool_w", [L, 4, 64, 64])
    pscale_d = din("pscale", [128, L * 2])
    qkg_d = din("qkg", [128, L * 2])
    fg_d = din("fg", [128, 8])
    ident_d = din("ident", [128, 128])
    rotm_d = din("rotm", [128, 128])
    hm_d = din("hm", [128, 128])
    cos_d = din("cosT", [128, T])
    sin_d = din("sinT", [128, T])
    rmask_d = din("rmask", [128, 12 * 8])
    pcnt_d = din("pcnt", [128, 2 * 2 * 16])
    out_d = nc.dram_tensor("out", [NXT, D], F32, kind="ExternalOutput").ap()
    QKn = Tl(nc.dram_tensor("s_qkn", [4, 128, T], BF16, kind="Internal").ap())
    Usc = Tl(nc.dram_tensor("s_u", [2, 128, T], F32, kind="Internal").ap())
    QCs = Tl(nc.dram_tensor("s_qc", [4, 128, T], BF16, kind="Internal").ap())
    KCs = Tl(nc.dram_tensor("s_kc", [128, T], BF16, kind="Internal").ap())
    VAs = Tl(nc.dram_tensor("s_va", [18, 128, 6, 128], BF16, kind="Internal").ap())
    OTs = Tl(nc.dram_tensor("s_ot", [8, 128, T], BF16, kind="Internal").ap())
    if dbg:
        hd_d = nc.dram_tensor("hdump", [128, 8, T], F32, kind="ExternalOutput").ap()

    with ExitStack() as es:
        kb = KB(nc, es)

        uid = [0]

        def sbt(st, name, shape, dt):
            uid[0] += 1
            return Tl(st.enter_context(nc.sbuf_tensor("sb%d_%s" % (uid[0], name), shape, dt)))

        ps = Tl(es.enter_context(nc.psum_tensor("ps", [128, 8, 512], F32)))

        def bank(b):
            return ps.buf(b)

        h = sbt(es, "h", [128, 8, T], F32)
        wu = sbt(es, "wu", [128, 2, 8, 2, 256], BF16)
        wd = sbt(es, "wd", [128, 2, 2, D], BF16)
        mw = sbt(es, "mw", [128, 2, 8, 256], BF16)
        modraw = sbt(es, "modraw", [128, 72, 2], F32)
        modv = sbt(es, "modv", [128, 2, 72], F32)
        mder = sbt(es, "mder", [128, 2, 3, 24], F32)
        sc = sbt(es, "sc", [128, 8, 2], BF16)
        ccs = sbt(es, "ccs", [128, 8, 2], F32)
        bada = sbt(es, "bada", [128, L * 72], F32)
        normg = sbt(es, "normg", [128, L * 24], F32)
        pscale = sbt(es, "pscale", [128, L * 2], F32)
        qkg = sbt(es, "qkg", [128, L * 2], F32)
        fg = sbt(es, "fg", [128, 8], F32)
        ident = sbt(es, "ident", [128, 128], F32)
        rotm = sbt(es, "rotm", [128, 128], F32)
        hm = sbt(es, "hm", [128, 128], BF16)
        ones_bf = sbt(es, "ones_bf", [128, 128], BF16)
        rmask = sbt(es, "rmask", [128, 12, 8], BF16)
        prc = sbt(es, "prc", [128, 2, 2, 16], F32)
        epsc = sbt(es, "epsc", [128, 1], F32)

        def hb(s, bi):
            return h.buf((s, bi))

        for tl, dr in [(bada, bada_d), (normg, normg_d), (pscale, pscale_d), (qkg, qkg_d), (fg, fg_d),
                       (ident, ident_d), (rotm, rotm_d), (ccs, cc_d)]:
            kb.dma("sp", tl[:], dr, writes=[tl.buf()])
        kb.dma("pool", hm[:], hm_d, writes=[hm.buf()])
        kb.dma("pool", rmask[:].rearrange("p a b -> p (a b)"), rmask_d, writes=[rmask.buf()])
        kb.dma("sp", prc[:].rearrange("p a b c -> p (a b c)"), pcnt_d, writes=[prc.buf()])
        kb.op("dve", lambda e: e.reciprocal(prc[:], prc[:]), reads=[prc.buf()], writes=[prc.buf()])
        kb.op("dve", lambda e: e.memset(ones_bf[:], 1.0), writes=[ones_bf.buf()])
        kb.op("dve", lambda e: e.memset(epsc[:], EPS), writes=[epsc.buf()])
        kb.op("act", lambda e: e.activation(sc[:], ccs[:], AF.Silu), reads=[ccs.buf()], writes=[sc.buf()])

        def mod_dma(l, pi):
            slot = pi % 2
            src = wada_d[l].rearrange("(k p) n -> p k n", p=128)[:, :, pi * 256:(pi + 1) * 256]
            kb.dma("pool", mw[:, slot], src, writes=[mw.buf(slot)])

        def mod_mm(l, pi):
            slot = pi % 2
            for c in range(2):
                for k in range(8):
                    kb.op("pe", lambda e, c=c, k=k: e.matmul(ps[:, 7, c * 2:c * 2 + 2],
                                                              mw[:, slot, k, c * 128:(c + 1) * 128], sc[:, k, :],
                                                              start=(k == 0), stop=(k == 7)),
                          reads=[mw.buf(slot), sc.buf()], writes=[bank(7)], inc=(c == 1 and k == 7))
            kb.op("dve", lambda e: e.tensor_copy(modraw[:, 2 * pi:2 * pi + 2, :],
                                                 ps[:, 7, 0:4].rearrange("p (a b) -> p a b", a=2)),
                  reads=[bank(7)], writes=[modraw.buf()])

        def emit_mod_piece(l, pi):
            mod_dma(l, pi)
            mod_mm(l, pi)

        def finalize_mod(l):
            for v in range(2):
                kb.op("dve", lambda e, v=v: e.tensor_tensor(modv[:, v, :], modraw[:, :, v], bada[:, l * 72:(l + 1) * 72],
                                                            ALU.add),
                      reads=[modraw.buf(), bada.buf()], writes=[modv.buf()])
            for v in range(2):
                for n in range(3):
                    kb.op("dve", lambda e, v=v, n=n: e.scalar_tensor_tensor(
                        mder[:, v, 0, n * 8:(n + 1) * 8], modv[:, v, (3 * n + 1) * 8:(3 * n + 2) * 8], 1.0,
                        normg[:, (l * 3 + n) * 8:(l * 3 + n + 1) * 8], ALU.add, ALU.mult),
                        reads=[modv.buf(), normg.buf()], writes=[mder.buf()])
                    kb.op("dve", lambda e, v=v, n=n: e.tensor_copy(mder[:, v, 1, n * 8:(n + 1) * 8],
                                                                   modv[:, v, (3 * n) * 8:(3 * n + 1) * 8]),
                          reads=[modv.buf()], writes=[mder.buf()])
                    kb.op("dve", lambda e, v=v, n=n: e.tensor_scalar(
                        mder[:, v, 2, n * 8:(n + 1) * 8], modv[:, v, (3 * n + 2) * 8:(3 * n + 3) * 8],
                        (1.0 if n == 1 else 0.5), None, ALU.mult),
                        reads=[modv.buf()], writes=[mder.buf()])

        def mscal(bi, kind, n, s):
            v = 1 if bi == 4 else 0
            return mder[:, v, kind, n * 8 + s:n * 8 + s + 1]

        with ExitStack() as ph:
            xt = sbt(ph, "xt", [128, 2, D], F32)
            for tt in range(18):
                sl = tt % 2
                kb.dma("sp", xt[:, sl, :], xin[tt * 128:(tt + 1) * 128, :], writes=[xt.buf(sl)])
                for hf in range(2):
                    bk = (tt * 2 + hf) % 6
                    for s4 in range(4):
                        s = hf * 4 + s4
                        kb.op("pe", lambda e, s=s, s4=s4, bk=bk: e.transpose(ps[:, bk, s4 * 128:(s4 + 1) * 128],
                                                                             xt[:, sl, s * 128:(s + 1) * 128], ident[:]),
                              reads=[xt.buf(sl), ident.buf()], writes=[bank(bk)], inc=(s4 == 3))
                    bi = min(tt // 4, 4)
                    eng = "dve" if hf == 0 else "act"
                    outap = h[:, hf * 4:hf * 4 + 4, tt * 128:(tt + 1) * 128]
                    inap = ps[:, bk, :].rearrange("p (a b) -> p a b", a=4)
                    if eng == "dve":
                        kb.op("dve", lambda e, o=outap, i=inap: e.tensor_copy(o, i), reads=[bank(bk)],
                              writes=[hb(s, bi) for s in range(hf * 4, hf * 4 + 4)])
                    else:
                        kb.op("act", lambda e, o=outap, i=inap: e.activation(o, i, AF.Copy), reads=[bank(bk)],
                              writes=[hb(s, bi) for s in range(hf * 4, hf * 4 + 4)])
            for pi in range(36):
                emit_mod_piece(0, pi)
            finalize_mod(0)
            kb.barrier()

        def stats(st, n_blocks):
            rstd = sbt(st, "rstd", [128, T], F32)
            sq = sbt(st, "sq", [128, 1, 8, 512], BF16)
            for bi in range(n_blocks):
                t0, w = BLKS[bi]
                sl = 0
                kb.op("act", lambda e: e.activation(sq[:, sl, :, 0:w], h[:, :, t0:t0 + w], AF.Square),
                      reads=[hb(s, bi) for s in range(8)], writes=[sq.buf(sl)])
                bk = 4 + bi % 2
                for k in range(8):
                    kb.op("pe", lambda e, k=k: e.matmul(ps[:, bk, 0:w], ones_bf[:], sq[:, sl, k, 0:w],
                                                        start=(k == 0), stop=(k == 7)),
                          reads=[sq.buf(sl), ones_bf.buf()], writes=[bank(bk)], inc=(k == 7))
                kb.op("dve", lambda e: e.tensor_scalar(rstd[:, t0:t0 + w], ps[:, bk, 0:w], 1.0 / D, EPS, ALU.mult, ALU.add),
                      reads=[bank(bk)], writes=[rstd.buf(bi)])
            tw = BLKS[n_blocks - 1][0] + BLKS[n_blocks - 1][1]
            rb = [rstd.buf(bi) for bi in range(n_blocks)]
            kb.op("act", lambda e: e.activation(rstd[:, 0:tw], rstd[:, 0:tw], AF.Sqrt), reads=rb, writes=rb)
            return rstd

        def recip_block(rstd, bi):
            t0, w = BLKS[bi]
            kb.op("dve", lambda e: e.reciprocal(rstd[:, t0:t0 + w], rstd[:, t0:t0 + w]),
                  reads=[rstd.buf(bi)], writes=[rstd.buf(bi)])

        def norm_block(bi, n, rstd, dst_fn, dst_bufs, tmp):
            t0, w = BLKS[bi]
            for s in range(8):
                sl = s % 2
                kb.op("dve", lambda e, s=s, sl=sl: e.scalar_tensor_tensor(
                    tmp[:, sl, 0:w], h[:, s, t0:t0 + w], mscal(bi, 0, n, s), rstd[:, t0:t0 + w], ALU.mult, ALU.mult),
                    reads=[hb(s, bi), rstd.buf(bi), mder.buf()], writes=[tmp.buf(sl)])
                kb.op("act", lambda e, s=s, sl=sl: e.activation(dst_fn(s), tmp[:, sl, 0:w], AF.Identity,
                                                                bias=mscal(bi, 1, n, s), scale=1.0),
                      reads=[tmp.buf(sl), mder.buf()], writes=[dst_bufs(s)])

        prefetched = set()

        def prefetch_ffn(l_, which_):
            nup = (f1u_d if which_ == 1 else f2u_d)[l_].rearrange("(k p) n -> p k n", p=128)
            ndn = (f1d_d if which_ == 1 else f2d_d)[l_].rearrange("(c p) n -> p c n", p=128)
            for gi in range(2):
                slot = gi % 2
                kb.dma("pool", wu[:, slot, :, 0, :], nup[:, :, gi * 256:(gi + 1) * 256], writes=[wu.buf((slot, 0))])
                kb.dma("pool", wu[:, slot, :, 1, :], nup[:, :, DFF + gi * 256:DFF + (gi + 1) * 256],
                       writes=[wu.buf((slot, 1))])
                kb.dma("pool", wd[:, slot], ndn[:, 2 * gi:2 * gi + 2, :], writes=[wd.buf(slot)])
            prefetched.add((l_, which_))

        def ffn_phase(l, which, n_blocks, next_mod_layer):
            n = 0 if which == 1 else 2
            up_d = (f1u_d if which == 1 else f2u_d)[l].rearrange("(k p) n -> p k n", p=128)
            dn_d = (f1d_d if which == 1 else f2d_d)[l].rearrange("(c p) n -> p c n", p=128)
            with ExitStack() as ph:
                rstd = stats(ph, n_blocks)
                xn = sbt(ph, "xn", [128, 8, T], BF16)
                tmp = sbt(ph, "ntmp", [128, 2, 512], F32)
                act = sbt(ph, "act", [128, 2, 2, 512], BF16)
                sil = sbt(ph, "sil", [128, 2, 512], F32)
                for bi in range(n_blocks):
                    recip_block(rstd, bi)
                for bi in range(n_blocks):
                    t0, w = BLKS[bi]
                    norm_block(bi, n, rstd, lambda s: xn[:, s, t0:t0 + w], lambda s: xn.buf((s, bi)), tmp)

                def load_group(gi, up_=None, dn_=None):
                    up_ = up_d if up_ is None else up_
                    dn_ = dn_d if dn_ is None else dn_
                    slot = gi % 2
                    kb.dma("pool", wu[:, slot, :, 0, :], up_[:, :, gi * 256:(gi + 1) * 256], writes=[wu.buf((slot, 0))])
                    kb.dma("pool", wu[:, slot, :, 1, :], up_[:, :, DFF + gi * 256:DFF + (gi + 1) * 256],
                           writes=[wu.buf((slot, 1))])
                    kb.dma("pool", wd[:, slot], dn_[:, 2 * gi:2 * gi + 2, :], writes=[wd.buf(slot)])

                steps = [(gi, bi) for gi in range(11) for bi in range(n_blocks)]

                def up_gen(i):
                    gi, bi = steps[i]
                    slot = gi % 2
                    t0, w = BLKS[bi]
                    pp = i % 2
                    for c in range(2):
                        for ab in range(2):
                            bk = 2 * c + ab
                            for k in range(8):
                                kb.op("pe", lambda e, k=k, ab=ab, c=c, bk=bk: e.matmul(
                                    ps[:, bk, 0:w], wu[:, slot, k, ab, c * 128:(c + 1) * 128], xn[:, k, t0:t0 + w],
                                    start=(k == 0), stop=(k == 7)),
                                    reads=[wu.buf((slot, ab)), xn.buf((k, bi))], writes=[bank(bk)], inc=(k == 7))
                                if k == 3:
                                    yield
                            if ab == 0:
                                kb.op("act", lambda e, c=c: e.activation(sil[:, c, 0:w], ps[:, 2 * c, 0:w], AF.Silu),
                                      reads=[bank(2 * c)], writes=[sil.buf(c)])
                            else:
                                kb.op("dve", lambda e, c=c: e.tensor_tensor(act[:, pp, c, 0:w], sil[:, c, 0:w],
                                                                            ps[:, 2 * c + 1, 0:w], ALU.mult),
                                      reads=[sil.buf(c), bank(2 * c + 1)], writes=[act.buf((pp, c))])
                            yield

                def down_tile(i, ds):
                    gi, bi = steps[i]
                    slot = gi % 2
                    t0, w = BLKS[bi]
                    pp = i % 2
                    bk = 4 + ds % 3
                    for c in range(2):
                        kb.op("pe", lambda e, c=c: e.matmul(
                            ps[:, bk, 0:w], wd[:, slot, c, ds * 128:(ds + 1) * 128], act[:, pp, c, 0:w],
                            start=(c == 0), stop=(c == 1)),
                            reads=[wd.buf(slot), act.buf((pp, c))], writes=[bank(bk)], inc=(c == 1))
                    kb.op("dve", lambda e: e.scalar_tensor_tensor(
                        h[:, ds, t0:t0 + w], ps[:, bk, 0:w], mscal(bi, 2, n, ds), h[:, ds, t0:t0 + w], ALU.mult, ALU.add),
                        reads=[bank(bk), hb(ds, bi), mder.buf()], writes=[hb(ds, bi)])

                def emit_up(i):
                    for _ in up_gen(i):
                        pass

                if (l, which) not in prefetched:
                    load_group(0)
                    load_group(1)
                mp = 0
                mbase = (0 if which == 1 else 18)
                if next_mod_layer is not None:
                    mod_dma(next_mod_layer, mbase + 0)
                    mod_dma(next_mod_layer, mbase + 1)
                emit_up(0)
                for i in range(len(steps)):
                    gi, bi = steps[i]
                    ug = up_gen(i + 1) if i + 1 < len(steps) else iter(())
                    next(ug, None)
                    next(ug, None)
                    for ds in range(8):
                        down_tile(i, ds)
                        next(ug, None)
                    for _ in ug:
                        pass
                    if bi == n_blocks - 1:
                        if gi + 2 < 11:
                            load_group(gi + 2)
                        if next_mod_layer is not None:
                            for _ in range(2):
                                if mp < 18:
                                    mod_mm(next_mod_layer, mbase + mp)
                                    if mp + 2 < 18:
                                        mod_dma(next_mod_layer, mbase + mp + 2)
                                    mp += 1
                if next_mod_layer is not None:
                    while mp < 18:
                        mod_mm(next_mod_layer, mbase + mp)
                        if mp + 2 < 18:
                            mod_dma(next_mod_layer, mbase + mp + 2)
                        mp += 1
                nxt = None if which == 1 else ((l + 1, 1) if l + 1 < n_layers else None)
                if nxt is not None and stop_after is None:
                    nup = (f1u_d if nxt[1] == 1 else f2u_d)[nxt[0]].rearrange("(k p) n -> p k n", p=128)
                    ndn = (f1d_d if nxt[1] == 1 else f2d_d)[nxt[0]].rearrange("(c p) n -> p c n", p=128)
                    load_group(0, nup, ndn)
                    load_group(1, nup, ndn)
                    prefetched.add(nxt)
                kb.barrier()

        def attend(qT, q0, w, steps, kT, vaug, scale, ostage, out_dram, out_buf, obase, mask_fn, P, Pbuf_i, tick=None, pending=None):
            base = Pbuf_i[0]
            ob = obase
            ns = len(steps)

            def s_step(si):
                kt = steps[si][0]
                sb_ = ((base + si) % 2) * 2
                for hh in range(2):
                    pr = slice(64 * hh, 64 * hh + 64)
                    kb.op("pe", lambda e, hh=hh, pr=pr: e.matmul(ps[:, sb_ + hh, 0:w], kT[pr, kt * 128:(kt + 1) * 128],
                                                                 qT[pr, q0:q0 + w], start=True, stop=True),
                          reads=[kT.buf(), qT.buf()], writes=[bank(sb_ + hh)], inc=True)

            s_step(0)
            for si, (kt, mi) in enumerate(steps):
                if si + 1 < ns:
                    s_step(si + 1)
                sb_ = ((base + si) % 2) * 2
                pp = (base + si) % 2
                kb.op("act", lambda e: e.activation(P[:, pp, :, 0:w], ps[:, sb_:sb_ + 2, 0:w], AF.Exp, scale=scale),
                      reads=[bank(sb_), bank(sb_ + 1)], writes=[P.buf(pp)])
                if mi is not None:
                    mask_fn(mi, P, pp, w)
                if tick is not None:
                    tick(sb_)
                if pending is not None and pending and si == min(2, ns - 1):
                    pending.pop(0)()
                for hh in range(2):
                    kb.op("pe", lambda e, hh=hh: e.matmul(ps[:, ob + hh, 0:w], vaug[:, kt, hh, :], P[:, pp, hh, 0:w],
                                                          start=(si == 0), stop=(si == ns - 1)),
                          reads=[vaug.buf(), P.buf(pp)], writes=[bank(ob + hh)], inc=(si == ns - 1))
            sidx = base + ns
            Pbuf_i[0] = sidx

            def fin():
                for hh in range(2):
                    o_pr = slice(64 * hh, 64 * hh + 64)
                    d_pr = slice(64 * (1 - hh), 64 * (1 - hh) + 64)
                    rd = ostage["rd"]
                    kb.op("dve", lambda e, hh=hh, d_pr=d_pr: e.reciprocal(rd[d_pr, hh, 0:w], ps[d_pr, ob + hh, 0:w]),
                          reads=[bank(ob + hh)], writes=[rd.buf(hh)])
                    os_ = ostage["o"]
                    oi = ostage["i"]
                    kb.op("dve", lambda e, hh=hh, o_pr=o_pr, d_pr=d_pr: e.tensor_tensor(
                        os_[o_pr, oi, 0:w], ps[o_pr, ob + hh, 0:w], rd[d_pr, hh, 0:w], ALU.mult),
                        reads=[bank(ob + hh), rd.buf(hh)], writes=[os_.buf((oi, hh))])
                oi = ostage["i"]
                kb.dma("sp", out_dram[:, q0:q0 + w], ostage["o"][:, oi, 0:w],
                       reads=[ostage["o"].buf((oi, 0)), ostage["o"].buf((oi, 1))], writes=[out_buf])
                ostage["i"] = (oi + 1) % 2

            if pending is None:
                fin()
            else:
                pending.append(fin)

        def mixer_phase(l, last):
            n_blocks = 5
            nqb = 4 if last else 5
            with ExitStack() as ph:
                rstd = stats(ph, n_blocks)
                win = sbt(ph, "win", [128, 8, 1792], BF16)
                cosT = sbt(ph, "cosT", [128, 512], F32)
                sinT = sbt(ph, "sinT", [128, 512], F32)
                aT = sbt(ph, "aT", [128, 8, 512], BF16)
                tmp = sbt(ph, "ntmp", [128, 2, 512], F32)
                stg = sbt(ph, "stg", [128, 2, 512], F32)
                stgb = sbt(ph, "stgb", [128, 4, 512], BF16)
                vst = sbt(ph, "vst", [128, 2, 6, 128], BF16)
                sqh = sbt(ph, "sqh", [128, 2, 512], BF16)
                rr = sbt(ph, "rr", [128, 2, 512], F32)
                qn = sbt(ph, "qn", [128, 2, 512], F32)
                qg = sbt(ph, "qg", [128, 2, 512], F32)
                sgc = [0, 0]
                vic = [0]
                wsrc = win_d[l].rearrange("(k p) n -> p k n", p=128)
                for pc in range(4):
                    kb.dma("pool", win[:, :, pc * 448:(pc + 1) * 448], wsrc[:, :, pc * 448:(pc + 1) * 448],
                           writes=[win.buf(pc)])
                winb = [win.buf(pc) for pc in range(4)]
                if stop_after is None:
                    prefetch_ffn(l, 2)
                kb.op("dve", lambda e: e.memset(vst[:], 1.0), writes=[vst.buf(0), vst.buf(1)])
                for bi in range(n_blocks):
                    t0, w = BLKS[bi]
                    kb.dma("sp", cosT[:, 0:w], cos_d[:, t0:t0 + w], writes=[cosT.buf()])
                    kb.dma("sp", sinT[:, 0:w], sin_d[:, t0:t0 + w], writes=[sinT.buf()])
                    recip_block(rstd, bi)
                    norm_block(bi, 1, rstd, lambda s: aT[:, s, 0:w], lambda s: aT.buf(s), tmp)
                    aTb = [aT.buf(s) for s in range(8)]
                    simple_i = [0]

                    def P(si):
                        qk = si >= 6
                        par = (si - 6) % 2
                        if qk:
                            bk = par
                        else:
                            bk = 4 + simple_i[0] % 2
                            simple_i[0] += 1
                        for k in range(8):
                            kb.op("pe", lambda e, k=k: e.matmul(
                                ps[:, bk, 0:w], win[:, k, si * 128:(si + 1) * 128], aT[:, k, 0:w],
                                start=(k == 0), stop=(k == 7)),
                                reads=winb + aTb, writes=[bank(bk)], inc=(k == 7))
                        if si < 4:
                            sg = sgc[0] % 4
                            sgc[0] += 1
                            kb.op("act", lambda e: e.activation(stgb[:, sg, 0:w], ps[:, bk, 0:w], AF.Copy),
                                  reads=[bank(bk)], writes=[stgb.buf(sg)])
                            kb.dma("sp", QKn[si, :, t0:t0 + w], stgb[:, sg, 0:w], reads=[stgb.buf(sg)], writes=[QKn.buf(si)])
                        elif si < 6:
                            sg = sgc[1] % 2
                            sgc[1] += 1
                            kb.op("act", lambda e: e.activation(stg[:, sg, 0:w], ps[:, bk, 0:w], AF.Copy),
                                  reads=[bank(bk)], writes=[stg.buf(sg)])
                            kb.dma("sp", Usc[si - 4, :, t0:t0 + w], stg[:, sg, 0:w], reads=[stg.buf(sg)],
                                   writes=[Usc.buf(si - 4)])
                        else:
                            gcol = l * 2 + (0 if si < 10 else 1)
                            kb.op("act", lambda e: e.activation(sqh[:, par, 0:w], ps[:, bk, 0:w], AF.Square),
                                  reads=[bank(bk)], writes=[sqh.buf(par)])
                            kb.op("act", lambda e: e.activation(qg[:, par, 0:w], ps[:, bk, 0:w], AF.Identity,
                                                                scale=qkg[:, gcol:gcol + 1]),
                                  reads=[bank(bk), qkg.buf()], writes=[qg.buf(par)])

                    def H(si):
                        par = (si - 6) % 2
                        bk = 2 if par == 0 else 6
                        kb.op("pe", lambda e: e.matmul(ps[:, bk, 0:w], hm[:], sqh[:, par, 0:w], start=True, stop=True),
                              reads=[hm.buf(), sqh.buf(par)], writes=[bank(bk)])
                        kb.op("act", lambda e: e.activation(rr[:, par, 0:w], ps[:, bk, 0:w], AF.Sqrt, bias=epsc[:, 0:1]),
                              reads=[bank(bk), epsc.buf()], writes=[rr.buf(par)])
                        kb.op("dve", lambda e: e.reciprocal(rr[:, par, 0:w], rr[:, par, 0:w]), reads=[rr.buf(par)],
                              writes=[rr.buf(par)])
                        kb.op("dve", lambda e: e.tensor_tensor(qn[:, par, 0:w], qg[:, par, 0:w], rr[:, par, 0:w], ALU.mult),
                              reads=[qg.buf(par), rr.buf(par)], writes=[qn.buf(par)])

                    def R(si):
                        par = (si - 6) % 2
                        bk = 3 if par == 0 else 7
                        kb.op("pe", lambda e: e.matmul(ps[:, bk, 0:w], rotm[:], qn[:, par, 0:w], start=True, stop=True),
                              reads=[rotm.buf(), qn.buf(par)], writes=[bank(bk)])
                        kb.op("dve", lambda e: e.tensor_tensor(rr[:, par, 0:w], ps[:, bk, 0:w], sinT[:, 0:w], ALU.mult),
                              reads=[bank(bk), sinT.buf()], writes=[rr.buf(par)])
                        kb.op("dve", lambda e: e.tensor_tensor(qn[:, par, 0:w], qn[:, par, 0:w], cosT[:, 0:w], ALU.mult),
                              reads=[qn.buf(par), cosT.buf()], writes=[qn.buf(par)])
                        sg = sgc[0] % 4
                        sgc[0] += 1
                        kb.op("dve", lambda e: e.tensor_tensor(stgb[:, sg, 0:w], qn[:, par, 0:w], rr[:, par, 0:w], ALU.add),
                              reads=[qn.buf(par), rr.buf(par)], writes=[stgb.buf(sg)])
                        if si < 10:
                            kb.dma("sp", QCs[si - 6, :, t0:t0 + w], stgb[:, sg, 0:w], reads=[stgb.buf(sg)],
                                   writes=[QCs.buf(si - 6)])
                        else:
                            kb.dma("sp", KCs[:, t0:t0 + w], stgb[:, sg, 0:w], reads=[stgb.buf(sg)], writes=[KCs.buf()])

                    def V(tl_):
                        tt = t0 // 128 + tl_
                        bk = 4 + simple_i[0] % 2
                        simple_i[0] += 1
                        for k in range(8):
                            kb.op("pe", lambda e, k=k: e.matmul(
                                ps[:, bk, 0:384], aT[:, k, tl_ * 128:(tl_ + 1) * 128], win[:, k, 1408:1792],
                                start=(k == 0), stop=(k == 7)),
                                reads=winb + aTb, writes=[bank(bk)], inc=(k == 7))
                        vs = vic[0] % 2
                        vic[0] += 1
                        pv = ps[:, bk, 0:384].rearrange("p (a b c) -> p a b c", a=3, b=2)
                        vv = vst[:, vs].rearrange("p (a b) c -> p a b c", b=2)
                        kb.op("act", lambda e: e.activation(vv[:, :, 0, 0:64], pv[:, :, 0, :], AF.Copy),
                              reads=[bank(bk)], writes=[vst.buf(vs)])
                        kb.op("act", lambda e: e.activation(vv[:, :, 1, 64:128], pv[:, :, 1, :], AF.Copy),
                              reads=[bank(bk)], writes=[vst.buf(vs)])
                        kb.dma("sp", VAs[tt], vst[:, vs], reads=[vst.buf(vs)], writes=[VAs.buf()])

                    nv = w // 128
                    tasks = [(P, 6), (P, 0), (H, 6), (P, 7), (P, 1), (H, 7), (P, 8), (R, 6), (P, 2), (H, 8), (P, 9), (R, 7),
                             (P, 3), (H, 9), (P, 10), (R, 8), (P, 4), (H, 10), (P, 5), (R, 9), (V, 0), (R, 10)]
                    tasks += [(V, i) for i in range(1, nv)]
                    for fn_, arg in tasks:
                        fn_(arg)
                kb.barrier()

            with ExitStack() as ph:
                kT2 = [sbt(ph, "kT%d" % i, [128, T], BF16) for i in range(2)]
                qT2n = [sbt(ph, "qT%d" % i, [128, T], BF16) for i in range(2)]
                vaug2 = [sbt(ph, "vaug%d" % i, [128, 18, 2, 128], BF16) for i in range(2)]
                brw2 = [sbt(ph, "brw%d" % i, [128, 2, 15, 64], F32) for i in range(2)]
                ECt2 = [sbt(ph, "ECt%d" % i, [128, 2, 22, 64], BF16) for i in range(2)]
                Tt2 = [sbt(ph, "Tt%d" % i, [128, 2, 22, 64], BF16) for i in range(2)]
                cur = [0]
                pend = []
                P = sbt(ph, "P", [128, 2, 2, 512], BF16)
                rd = sbt(ph, "rd", [128, 2, 512], F32)
                osg = sbt(ph, "osg", [128, 2, 512], BF16)
                ostage = {"rd": rd, "o": osg, "i": 0}
                Pi = [0]
                obank = [0]

                def na_mask(mi, P_, pp, w):
                    kind, s0, ri = mi
                    tab = Tt2[cur[0]] if kind == "T" else ECt2[cur[0]]
                    pv = P_[:, pp].rearrange("p a (b c) -> p a b c", b=8)
                    kb.op("dve", lambda e: e.tensor_tensor(pv, pv, tab[:, :, s0:s0 + 8, :], ALU.mult),
                          reads=[P_.buf(pp), tab.buf()], writes=[P_.buf(pp)])
                    if kind == "E":
                        rmv = rmask[:, ri, :].unsqueeze(1).unsqueeze(3).broadcast_to([128, 2, 8, 64])
                        kb.op("dve", lambda e: e.tensor_tensor(pv, pv, rmv, ALU.mult),
                              reads=[P_.buf(pp), rmask.buf()], writes=[P_.buf(pp)])

                for s in range(2):
                    kT, qT, vaug, brw, ECt, Tt = kT2[s], qT2n[s], vaug2[s], brw2[s], ECt2[s], Tt2[s]
                    for hh in range(2):
                        for half in range(2):
                            kb.dma("sp", brw[64 * half:64 * half + 64, hh], rpbx_d[l, 2 * s + hh], writes=[brw.buf(hh)])
                    kb.dma("sp", kT[:], QKn[2 + s], reads=[QKn.buf(2 + s)], writes=[kT.buf()])
                    kb.dma("sp", qT[:], QKn[s], reads=[QKn.buf(s)], writes=[qT.buf()])
                    kb.dma("sp", vaug[:], VAs[:, :, 2 * s:2 * s + 2, :].rearrange("t p s c -> p t s c"),
                           reads=[VAs.buf()], writes=[vaug.buf()])
                    kb.op("dve", lambda e: e.memset(ECt[:], 0.0), writes=[ECt.buf()])
                    kb.op("dve", lambda e: e.memset(Tt[:], 0.0), writes=[Tt.buf()])
                    for hh in range(2):
                        for half in range(2):
                            pr = slice(64 * half, 64 * half + 64)
                            kb.op("act", lambda e, hh=hh, pr=pr, half=half: e.activation(
                                ECt[pr, hh, 3 + half:18 + half, :], brw[pr, hh, :, :], AF.Exp),
                                reads=[brw.buf(hh)], writes=[ECt.buf()])
                            kb.op("act", lambda e, hh=hh, pr=pr, half=half: e.activation(
                                Tt[pr, hh, 7 + half:15 + half, :], brw[pr, hh, 4:12, :], AF.Exp),
                                reads=[brw.buf(hh)], writes=[Tt.buf()])
                for s in range(2):
                    kT, qT, vaug = kT2[s], qT2n[s], vaug2[s]
                    cur[0] = s
                    for qb in range(nqb):
                        q0, w = BLKS[qb]
                        if qb < 4:
                            R = 8 * qb
                            krs = {0: range(0, 12, 2), 8: range(4, 20, 2), 16: range(12, 28, 2), 24: range(20, 32, 2)}[R]
                            steps = []
                            for ti, kr in enumerate(krs):
                                s0 = 10 - (kr - R)
                                if R in (8, 16):
                                    steps.append((kr // 2, ("T", s0, 0)))
                                else:
                                    steps.append((kr // 2, ("E", s0, (0 if R == 0 else 6) + ti)))
                            steps += [(16, None), (17, None)]
                        else:
                            steps = [(16, None), (17, None)]
                        ob = 4 + 2 * (obank[0] % 2)
                        obank[0] += 1
                        attend(qT, q0, w, steps, kT, vaug, 0.125, ostage, OTs[s], OTs.buf(s), ob, na_mask, P, Pi, pending=pend)
                while pend:
                    pend.pop(0)()
                kb.barrier()

            with ExitStack() as ph:
                kT = sbt(ph, "kTc", [128, T], BF16)
                qT2 = [sbt(ph, "qTc%d" % i, [128, T], BF16) for i in range(2)]
                vaug = sbt(ph, "vaugc", [128, 18, 2, 128], BF16)
                P = sbt(ph, "Pc", [128, 2, 2, 512], BF16)
                rd = sbt(ph, "rdc", [128, 2, 512], F32)
                osg = sbt(ph, "osgc", [128, 2, 512], BF16)
                ostage = {"rd": rd, "o": osg, "i": 0}
                Pi = [0]
                obank = [0]
                pool_ops = []
                pend = []

                def rec_op(*a_, **k_):
                    pool_ops.append((kb.op, a_, k_))

                def rec_dma(*a_, **k_):
                    pool_ops.append((kb.dma, a_, k_))

                def tick(bk_=0, nmax=1):
                    for _ in range(nmax):
                        if pool_ops:
                            f_, a_, k_ = pool_ops.pop(0)
                            if f_ in (kb.op, kb.dma):
                                f_(*a_, **k_)
                            else:
                                f_(bk=bk_)

                PADN = NXT + 32
                X0 = sbt(ph, "pX0", [128, PADN], F32)
                X1 = sbt(ph, "pX1", [128, PADN], F32)
                X2 = sbt(ph, "pX2", [128, PADN], F32)
                yb = sbt(ph, "pyb", [128, NXT], BF16)
                et = sbt(ph, "pet", [128, 2, 8], F32)
                wbd = sbt(ph, "wbd", [128, 2, 128], BF16)
                ostg = sbt(ph, "postg", [128, 2, 512], BF16)
                rec_op("dve", lambda e: e.memset(wbd[:], 0.0), writes=[wbd.buf()])
                for s in range(2):
                    for g in range(2):
                        rec_dma("pool", wbd[64 * g:64 * g + 64, s, 64 * g:64 * g + 64], poolw_d[l, 2 * s + g], writes=[wbd.buf()])
                oic = [0]

                def pool_seg(s, seg):
                    if True:
                        n = NXT if seg == 0 else NCT
                        tb = 0 if seg == 0 else NXT
                        bufs = [X0.buf(), X1.buf(), X2.buf()]
                        rec_op("dve", lambda e: e.memset(X0[:, 0:16], 0.0), writes=[X0.buf()])
                        rec_op("dve", lambda e, n=n: e.memset(X0[:, 16 + n:32 + n], 0.0), writes=[X0.buf()])
                        rec_dma("sp", X0[:, 16:16 + n], Usc[s, :, tb:tb + n], reads=[Usc.buf(s)], writes=[X0.buf()])

                        def sa(dst, src, lo, hi, d1, d2, pr=slice(0, 128)):
                            rec_op("dve", lambda e: e.tensor_tensor(dst[pr, lo + 16:hi + 16], src[pr, lo + 16 + d1:hi + 16 + d1],
                                                                   src[pr, lo + 16 + d2:hi + 16 + d2], ALU.add),
                                  reads=[src.buf()], writes=[dst.buf()])

                        pa, pb = slice(0, 64), slice(64, 128)
                        sa(X1, X0, -14, n + 14, -1, 0)
                        if s == 0:
                            rec_op("dve", lambda e: e.tensor_copy(X2[pa, 16:16 + n], X1[pa, 16:16 + n]), reads=[X1.buf()],
                                  writes=[X2.buf()])
                            sa(X2, X1, 0, n, -1, 1, pb)
                            ws = (2, 4)
                        else:
                            sa(X2, X1, -12, n + 12, -1, 1)
                            sa(X1, X2, 0, n, -2, 2, pa)
                            sa(X1, X2, -8, n + 8, -2, 2, pb)
                            sa(X2, X1, 0, n, -4, 4, pb)
                            rec_op("dve", lambda e: e.tensor_copy(X2[pa, 16:16 + n], X1[pa, 16:16 + n]), reads=[X1.buf()],
                                  writes=[X2.buf()])
                            ws = (8, 16)
                        for g, pr in enumerate((pa, pb)):
                            rec_op("dve", lambda e, g=g, pr=pr: e.scalar_tensor_tensor(
                                yb[pr, 0:n], X2[pr, 16:16 + n], 1.0 / ws[g], X0[pr, 16:16 + n], ALU.mult, ALU.subtract),
                                reads=[X2.buf(), X0.buf()], writes=[yb.buf()])
                        for ed in range(2):
                            e0 = 0 if ed == 0 else n - 8
                            rec_op("dve", lambda e, ed=ed, e0=e0: e.tensor_tensor(
                                et[:, ed, :], X2[:, 16 + e0:24 + e0], prc[:, s, seg, ed * 8:ed * 8 + 8], ALU.mult),
                                reads=[X2.buf(), prc.buf()], writes=[et.buf(ed)])
                            rec_op("dve", lambda e, ed=ed, e0=e0: e.tensor_tensor(
                                yb[:, e0:e0 + 8], et[:, ed, :], X0[:, 16 + e0:24 + e0], ALU.subtract),
                                reads=[et.buf(ed), X0.buf()], writes=[yb.buf()])
                        for c0 in range(0, n, 512):
                            w = min(512, n - c0)
                            bk = 2 * ((c0 // 512) % 2)
                            o_ = oic[0] % 2
                            oic[0] += 1

                            def bundle(bk=0, c0=c0, w=w, o_=o_):
                                kb.op("pe", lambda e: e.matmul(ps[:, bk, 0:w], wbd[:, s, :], yb[:, c0:c0 + w],
                                                               start=True, stop=True),
                                      reads=[wbd.buf(), yb.buf()], writes=[bank(bk)])
                                kb.op("dve", lambda e: e.tensor_scalar(ostg[:, o_, 0:w], ps[:, bk, 0:w],
                                                                       pscale[:, l * 2 + s:l * 2 + s + 1], None, ALU.mult),
                                      reads=[bank(bk), pscale.buf()], writes=[ostg.buf(o_)])
                                kb.dma("sp", OTs[2 + s, :, tb + c0:tb + c0 + w], ostg[:, o_, 0:w], reads=[ostg.buf(o_)],
                                       writes=[OTs.buf(2 + s)])

                            pool_ops.append((bundle, (), {}))

                for s_ in range(2):
                    for seg_ in range(1 if last else 2):
                        pool_seg(s_, seg_)
                kb.dma("sp", kT[:], KCs[:], reads=[KCs.buf()], writes=[kT.buf()])
                kb.dma("sp", vaug[:], VAs[:, :, 4:6, :].rearrange("t p s c -> p t s c"), reads=[VAs.buf()],
                       writes=[vaug.buf()])
                kb.dma("sp", qT2[0][:], QCs[0], reads=[QCs.buf(0)], writes=[qT2[0].buf()])
                for j in range(4):
                    qT = qT2[j % 2]
                    if j + 1 < 4:
                        kb.dma("sp", qT2[(j + 1) % 2][:], QCs[j + 1], reads=[QCs.buf(j + 1)], writes=[qT2[(j + 1) % 2].buf()])
                    for qb in range(nqb):
                        q0, w = BLKS[qb]
                        steps = [(kt, None) for kt in (range(18) if qb < 4 else (16, 17))]
                        ob = 4 + 2 * (obank[0] % 2)
                        obank[0] += 1
                        attend(qT, q0, w, steps, kT, vaug, 0.125, ostage, OTs[4 + j], OTs.buf(4 + j), ob, None, P, Pi, tick=tick, pending=pend)
                while pend:
                    pend.pop(0)()
                tick(0, 100000)
                kb.barrier()

            with ExitStack() as ph:
                wo = sbt(ph, "wo", [128, 8, D], BF16)
                oT = sbt(ph, "oTb", [128, 2, 8, 512], BF16)
                kb.dma("pool", wo[:], wout_d[l].rearrange("(k p) n -> p k n", p=128), writes=[wo.buf()])
                otb = [OTs.buf(i) for i in range(8)]
                for bi in range(nqb):
                    t0, w = BLKS[bi]
                    sl = bi % 2
                    kb.dma("sp", oT[:, sl, :, 0:w], OTs[:, :, t0:t0 + w].rearrange("s p t -> p s t"), reads=otb,
                           writes=[oT.buf(sl)])
                    for ds in range(8):
                        bk = ds % 4
                        for k in range(8):
                            kb.op("pe", lambda e, k=k, ds=ds, bk=bk: e.matmul(
                                ps[:, bk, 0:w], wo[:, k, ds * 128:(ds + 1) * 128], oT[:, sl, k, 0:w],
                                start=(k == 0), stop=(k == 7)),
                                reads=[wo.buf(), oT.buf(sl)], writes=[bank(bk)], inc=(k == 7))
                        kb.op("dve", lambda e, ds=ds, bk=bk: e.scalar_tensor_tensor(
                            h[:, ds, t0:t0 + w], ps[:, bk, 0:w], mscal(bi, 2, 1, ds), h[:, ds, t0:t0 + w], ALU.mult, ALU.add),
                            reads=[bank(bk), hb(ds, bi), mder.buf()], writes=[hb(ds, bi)])
                kb.barrier()

        for l in range(n_layers):
            last = (l == L - 1)
            nm = l + 1 if l + 1 < n_layers else None
            ffn_phase(l, 1, 5, nm)
            if stop_after == (l, 1):
                break
            mixer_phase(l, last)
            if stop_after == (l, 2):
                break
            ffn_phase(l, 2, 4 if last else 5, nm)
            if nm is not None:
                finalize_mod(nm)
                kb.barrier()

        with ExitStack() as ph:
            if dbg:
                kb.dma("sp", hd_d, h[:], reads=[hb(s, bi) for s in range(8) for bi in range(5)])
            rstd = stats(ph, 4)
            yf = sbt(ph, "yf", [128, 2, 8, 512], F32)
            ot = sbt(ph, "ot", [128, 2, D], F32)
            oi = 0
            for bi in range(4):
                t0, w = BLKS[bi]
                sl = bi % 2
                recip_block(rstd, bi)
                for s in range(8):
                    kb.op("dve", lambda e, s=s: e.scalar_tensor_tensor(
                        yf[:, sl, s, :], h[:, s, t0:t0 + w], fg[:, s:s + 1], rstd[:, t0:t0 + w], ALU.mult, ALU.mult),
                        reads=[hb(s, bi), fg.buf(), rstd.buf(bi)], writes=[yf.buf((sl, s))])
                for tl_ in range(4):
                    o_ = oi % 2
                    oi += 1
                    for hf in range(2):
                        bk = (oi * 2 + hf) % 6
                        for s4 in range(4):
                            s = hf * 4 + s4
                            kb.op("pe", lambda e, s=s, s4=s4, bk=bk: e.transpose(
                                ps[:, bk, s4 * 128:(s4 + 1) * 128], yf[:, sl, s, tl_ * 128:(tl_ + 1) * 128], ident[:]),
                                reads=[yf.buf((sl, s)), ident.buf()], writes=[bank(bk)], inc=(s4 == 3))
                        if hf == 0:
                            kb.op("dve", lambda e, bk=bk, o_=o_: e.tensor_copy(ot[:, o_, 0:512], ps[:, bk, :]),
                                  reads=[bank(bk)], writes=[ot.buf((o_, 0))])
                        else:
                            kb.op("act", lambda e, bk=bk, o_=o_: e.activation(ot[:, o_, 512:1024], ps[:, bk, :], AF.Copy),
                                  reads=[bank(bk)], writes=[ot.buf((o_, 1))])
                    r0 = t0 + tl_ * 128
                    kb.dma("sp", out_d[r0:r0 + 128, :], ot[:, o_, :], reads=[ot.buf((o_, 0)), ot.buf((o_, 1))])
            kb.barrier()
    return nc


def _consts():
    ident = np.eye(128, dtype=np.float32)
    rotm = np.zeros((128, 128), np.float32)
    for m in range(128):
        d = m % 64
        q16 = d % 32
        partner = m + 16 if q16 < 16 else m - 16
        rotm[partner, m] = 1.0
    hm = np.zeros((128, 128), np.float32)
    hm[0:64, 0:64] = 1.0 / 64
    hm[64:128, 64:128] = 1.0 / 64
    inv_freq = (10000.0 ** (-np.arange(0, 32, 2, dtype=np.float32) / 32.0)).astype(np.float32)
    pos = np.arange(NXT)
    ang_row = (pos // 64).astype(np.float32)[:, None] * inv_freq[None, :]
    ang_col = (pos % 64).astype(np.float32)[:, None] * inv_freq[None, :]
    cosT = np.ones((128, T), np.float32)
    sinT = np.zeros((128, T), np.float32)
    for p in range(128):
        d = p % 64
        ang = ang_row if d < 32 else ang_col
        j = d % 16
        first = (d % 32) < 16
        cosT[p, :NXT] = np.cos(ang[:, j])
        sinT[p, :NXT] = (-1.0 if first else 1.0) * np.sin(ang[:, j])
    rmask = np.zeros((128, 12, 8), np.float32)
    for ci, (R, krs) in enumerate(((0, range(0, 12, 2)), (24, range(20, 32, 2)))):
        for ti, kr in enumerate(krs):
            for kl in range(2):
                for qr in range(8):
                    r = R + qr
                    r0 = min(max(r - 4, 0), 24)
                    if r0 <= kr + kl < r0 + 8:
                        rmask[64 * kl:64 * kl + 64, ci * 6 + ti, qr] = 1.0
    pcnt = np.ones((128, 2, 2, 16), np.float32)
    for s in range(2):
        for g in range(2):
            wdw = (2, 4, 8, 16)[2 * s + g]
            for seg, n in enumerate((NXT, NCT)):
                for ed in range(2):
                    for i in range(8):
                        t = i if ed == 0 else n - 8 + i
                        lo = max(t - wdw // 2, 0)
                        hi = min(t - wdw // 2 + wdw, n)
                        pcnt[64 * g:64 * g + 64, s, seg, ed * 8 + i] = hi - lo
    return dict(ident=ident, rotm=rotm, hm=hm, cosT=cosT, sinT=sinT,
                rmask=rmask.reshape(128, 96), pcnt=pcnt.reshape(128, 64))


def _rpb_expand(na_rpb):
    qc = np.arange(64)
    kc = np.arange(64)
    win_c0 = np.clip(qc - 8, 0, 48)
    valid = (kc[:, None] >= win_c0[None, :]) & (kc[:, None] < win_c0[None, :] + 16)
    dx = np.clip(kc[:, None] - qc[None, :], -15, 15) + 15
    g = na_rpb[:, :, ::-1, :][:, :, :, dx]
    g = np.where(valid[None, None, None], g, np.float32(NEGB)).astype(np.float32)
    return np.ascontiguousarray(g.transpose(0, 1, 3, 2, 4))


def _prep_shared(inp):
    f = lambda a: np.ascontiguousarray(np.asarray(a, dtype=np.float32))
    qperm = np.concatenate([np.arange(64) + 64 * hh for hh in (0, 4, 1, 5, 2, 6, 3, 7)])
    w_in = f(inp["w_in"])
    cols = np.concatenate([np.arange(0, 256), np.arange(256, 512), np.arange(768, 1024), 1024 + qperm,
                           np.arange(1536, 1664), np.arange(512, 768), np.arange(1664, 1792)])
    w_in_p = np.ascontiguousarray(w_in[:, :, cols])
    w_out = f(inp["w_out"])
    rows = np.concatenate([np.arange(0, 512), 512 + qperm])
    w_out_p = np.ascontiguousarray(w_out[:, rows, :])
    bada = f(inp["b_ada"]).reshape(L, 72, 128).transpose(2, 0, 1).reshape(128, L * 72)
    normg = f(inp["norm_g"]).reshape(L, 3, 8, 128).transpose(3, 0, 1, 2).reshape(128, L * 24)
    pscale = f(inp["pool_scale"]).reshape(L, 2, 128).transpose(2, 0, 1).reshape(128, L * 2)
    qg = np.tile(f(inp["q_norm_g"]), (1, 2))
    kg = np.tile(f(inp["k_norm_g"]), (1, 2))
    qkg = np.stack([qg, kg], axis=-1).transpose(1, 0, 2).reshape(128, L * 2)
    fg = f(inp["final_g"]).reshape(8, 128).T
    sh = dict(w_ada=f(inp["w_ada"]), bada=np.ascontiguousarray(bada), normg=np.ascontiguousarray(normg),
              ffn1_up=f(inp["ffn1_up"]), ffn1_down=f(inp["ffn1_down"]), ffn2_up=f(inp["ffn2_up"]),
              ffn2_down=f(inp["ffn2_down"]), w_in=w_in_p, w_out=w_out_p, rpbx=_rpb_expand(f(inp["na_rpb"])),
              pool_w=f(inp["pool_w"]), pscale=np.ascontiguousarray(pscale), qkg=np.ascontiguousarray(qkg),
              fg=np.ascontiguousarray(fg))
    sh.update(_consts())
    return sh


def _in_maps(inp, n=8):
    sh = _prep_shared(inp)
    x = np.asarray(inp["x"], np.float32)
    ctx = np.asarray(inp["ctx"], np.float32)
    c = np.asarray(inp["c"], np.float32)
    c_ctx = np.asarray(inp["c_ctx"], np.float32)
    maps = []
    for b in range(n):
        m = dict(sh)
        m["xin"] = np.ascontiguousarray(np.concatenate([x[b], ctx[b]], axis=0))
        m["cc"] = np.ascontiguousarray(np.stack([c[b].reshape(8, 128).T, c_ctx.reshape(8, 128).T], axis=-1))
        maps.append(m)
    return maps


def kernel(**inputs):
    nc = build_program()
    maps = _in_maps(inputs, 8)
    res = run_bass_kernel_spmd(nc, maps, core_ids=list(range(8)))
    return np.stack([np.asarray(r["out"], dtype=np.float32) for r in res.results], axis=0)
```
